# Optimizing a Trainium2 kernel written in Bass

```python
import math
import jax, jax.numpy as jnp
from jax import lax
import numpy as np

D_MODEL = 1024
BATCH = 4
SEQ = 8192
DEPTH = 1
DEC_BATCH = 8
DEC_SEQ = 8192
PAST_LEN = 128

GRID_W = 64
Q_BLOCK = 128
EPS = 1e-6
A_HEADS = 8
A_HEAD_DIM = 64
A_V_DIM = 2 * A_HEAD_DIM
A_ROT_DIM = A_HEAD_DIM // 4
A_ROPE_THETA = 500000.0
B_HEADS = 16
B_KV_HEADS = 4
B_GROUP = B_HEADS // B_KV_HEADS
B_HEAD_DIM = 64
B_ROPE_THETA = 10000.0
D_FF = 2816
CONV_W = 3

A_Q = A_HEADS * 2 * A_HEAD_DIM
A_K = A_HEADS * 2 * A_HEAD_DIM
A_V = A_HEADS * A_V_DIM
B_Q = B_HEADS * B_HEAD_DIM
B_K = B_KV_HEADS * B_HEAD_DIM
B_V = B_KV_HEADS * B_HEAD_DIM
GATE_W = 2 * D_MODEL
D_IN = A_Q + A_K + A_V + B_Q + B_K + B_V + GATE_W
SPLIT_IDX = (A_Q, A_Q + A_K, A_Q + A_K + A_V, A_Q + A_K + A_V + B_Q, A_Q + A_K + A_V + B_Q + B_K, A_Q + A_K + A_V + B_Q + B_K + B_V)

kernel_name = 'hybrid_diffattn_gqa2d_convffn_encoder'


def _rmsnorm(x, g):
    x32 = x.astype(jnp.float32)
    y = x32 * lax.rsqrt(jnp.mean(x32 * x32, axis=-1, keepdims=True) + EPS)
    return (y * g.astype(jnp.float32)).astype(x.dtype)


def _rope_angles(pos, dim, theta):
    inv = 1.0 / (theta ** (jnp.arange(0, dim, 2, dtype=jnp.float32) / dim))
    ang = pos.astype(jnp.float32)[:, None] * inv[None, :]
    return jnp.cos(ang), jnp.sin(ang)


def _rotate(x, cos, sin):
    c = cos[None, :, None, :].astype(x.dtype)
    s = sin[None, :, None, :].astype(x.dtype)
    x1, x2 = jnp.split(x, 2, axis=-1)
    return jnp.concatenate([x1 * c - x2 * s, x2 * c + x1 * s], axis=-1)


def _diff_attention(q, k, v, lam):
    B, S = q.shape[0], q.shape[1]
    nb = S // Q_BLOCK
    qb = q.reshape(B, nb, Q_BLOCK, A_HEADS, 2, A_HEAD_DIM).swapaxes(0, 1)
    scale = A_HEAD_DIM ** -0.5

    def block(qi):
        s = jnp.einsum('bqhcd,bkhcd->bhcqk', qi, k).astype(jnp.float32) * scale
        p = jax.nn.softmax(s, axis=-1)
        w = (p[:, :, 0] - lam * p[:, :, 1]).astype(v.dtype)
        return jnp.einsum('bhqk,bkhd->bqhd', w, v)

    o = lax.map(block, qb)
    return o.swapaxes(0, 1).reshape(B, S, A_HEADS, A_V_DIM)


def _gqa_attention(q, k, v):
    B, S = q.shape[0], q.shape[1]
    nb = S // Q_BLOCK
    qb = q.reshape(B, nb, Q_BLOCK, B_KV_HEADS, B_GROUP, B_HEAD_DIM).swapaxes(0, 1)
    scale = B_HEAD_DIM ** -0.5

    def block(qi):
        s = jnp.einsum('bqngd,bknd->bngqk', qi, k).astype(jnp.float32) * scale
        p = jax.nn.softmax(s, axis=-1).astype(v.dtype)
        return jnp.einsum('bngqk,bknd->bqngd', p, v)

    o = lax.map(block, qb)
    return o.swapaxes(0, 1).reshape(B, S, B_HEADS * B_HEAD_DIM)


def _layer(x, layer_idx, norm_mix_g, w_in, lam_q1, lam_k1, lam_q2, lam_k2, subln_g,
           q_norm_g, k_norm_g, w_proj_a, w_proj_b, w_out, norm_ffn_g, w_up, conv_w, conv_b, w_down):
    B, S, _ = x.shape
    rows_n = S // GRID_W
    h = _rmsnorm(x, norm_mix_g)
    proj = h @ w_in
    qa, ka, va, qg, kg, vg, gates = jnp.split(proj, SPLIT_IDX, axis=-1)

    pos = jnp.arange(S)
    cos_a, sin_a = _rope_angles(pos, A_ROT_DIM, A_ROPE_THETA)

    def partial_rope(t):
        return jnp.concatenate([_rotate(t[..., :A_ROT_DIM], cos_a, sin_a), t[..., A_ROT_DIM:]], axis=-1)

    qa = partial_rope(qa.reshape(B, S, 2 * A_HEADS, A_HEAD_DIM)).reshape(B, S, A_HEADS, 2, A_HEAD_DIM)
    ka = partial_rope(ka.reshape(B, S, 2 * A_HEADS, A_HEAD_DIM)).reshape(B, S, A_HEADS, 2, A_HEAD_DIM)
    va = va.reshape(B, S, A_HEADS, A_V_DIM)
    lam_init = 0.8 - 0.6 * math.exp(-0.3 * layer_idx)
    lam = (jnp.exp(jnp.sum(lam_q1.astype(jnp.float32) * lam_k1.astype(jnp.float32)))
           - jnp.exp(jnp.sum(lam_q2.astype(jnp.float32) * lam_k2.astype(jnp.float32))) + lam_init)
    oa = _diff_attention(qa, ka, va, lam)
    oa = (_rmsnorm(oa, subln_g) * (1.0 - lam_init)).reshape(B, S, A_HEADS * A_V_DIM)

    rows = jnp.repeat(jnp.arange(rows_n), GRID_W)
    cols = jnp.tile(jnp.arange(GRID_W), rows_n)
    half = B_HEAD_DIM // 2
    cos_r, sin_r = _rope_angles(rows, half, B_ROPE_THETA)
    cos_c, sin_c = _rope_angles(cols, half, B_ROPE_THETA)

    def axial_rope(t):
        return jnp.concatenate([_rotate(t[..., :half], cos_r, sin_r), _rotate(t[..., half:], cos_c, sin_c)], axis=-1)

    qg = axial_rope(_rmsnorm(qg.reshape(B, S, B_HEADS, B_HEAD_DIM), q_norm_g))
    qg = qg.reshape(B, S, B_KV_HEADS, B_GROUP, B_HEAD_DIM)
    kg = axial_rope(_rmsnorm(kg.reshape(B, S, B_KV_HEADS, B_HEAD_DIM), k_norm_g))
    vg = vg.reshape(B, S, B_KV_HEADS, B_HEAD_DIM)
    ob = _gqa_attention(qg, kg, vg)

    g_a, g_b = jnp.split(gates, 2, axis=-1)
    mix = jax.nn.sigmoid(g_a) * (oa @ w_proj_a) + jax.nn.sigmoid(g_b) * (ob @ w_proj_b)
    x = x + mix @ w_out

    h = _rmsnorm(x, norm_ffn_g)
    u = h @ w_up
    up = jnp.pad(u, ((0, 0), (1, 1), (0, 0)))
    u = up[:, :-2] * conv_w[0] + up[:, 1:-1] * conv_w[1] + up[:, 2:] * conv_w[2] + conv_b
    val, gate = jnp.split(u, 2, axis=-1)
    return x + (jax.nn.silu(gate) * val) @ w_down


def _trunk(x, norm_mix_g, w_in, lam_q1, lam_k1, lam_q2, lam_k2, subln_g, q_norm_g, k_norm_g,
           w_proj_a, w_proj_b, w_out, norm_ffn_g, w_up, conv_w, conv_b, w_down, norm_final_g):
    for l in range(DEPTH):
        x = _layer(x, l, norm_mix_g[l], w_in[l], lam_q1[l], lam_k1[l], lam_q2[l], lam_k2[l], subln_g[l],
                   q_norm_g[l], k_norm_g[l], w_proj_a[l], w_proj_b[l], w_out[l], norm_ffn_g[l],
                   w_up[l], conv_w[l], conv_b[l], w_down[l])
    return _rmsnorm(x, norm_final_g)


def setup_inputs(seed: int = 0) -> dict:
    key = jax.random.key(seed)
    ks = jax.random.split(key, 20)
    f32 = jnp.float32

    def nrm(k, shape, scale):
        return jax.random.normal(k, shape, f32) * scale

    def gain(k, shape):
        return 1.0 + 0.02 * jax.random.normal(k, shape, f32)

    return {
        'x_prompt': nrm(ks[0], (BATCH, SEQ, D_MODEL), 1.0),
        'x_sample': nrm(ks[1], (DEC_BATCH, DEC_SEQ, D_MODEL), 1.0),
        'norm_mix_g': gain(ks[2], (DEPTH, D_MODEL)),
        'w_in': nrm(ks[3], (DEPTH, D_MODEL, D_IN), D_MODEL ** -0.5),
        'lam_q1': nrm(ks[4], (DEPTH, A_HEAD_DIM), 0.1),
        'lam_k1': nrm(ks[5], (DEPTH, A_HEAD_DIM), 0.1),
        'lam_q2': nrm(ks[6], (DEPTH, A_HEAD_DIM), 0.1),
        'lam_k2': nrm(ks[7], (DEPTH, A_HEAD_DIM), 0.1),
        'subln_g': gain(ks[8], (DEPTH, A_V_DIM)),
        'q_norm_g': gain(ks[9], (DEPTH, B_HEAD_DIM)),
        'k_norm_g': gain(ks[10], (DEPTH, B_HEAD_DIM)),
        'w_proj_a': nrm(ks[11], (DEPTH, A_HEADS * A_V_DIM, D_MODEL), (A_HEADS * A_V_DIM) ** -0.5),
        'w_proj_b': nrm(ks[12], (DEPTH, B_HEADS * B_HEAD_DIM, D_MODEL), (B_HEADS * B_HEAD_DIM) ** -0.5),
        'w_out': nrm(ks[13], (DEPTH, D_MODEL, D_MODEL), D_MODEL ** -0.5),
        'norm_ffn_g': gain(ks[14], (DEPTH, D_MODEL)),
        'w_up': nrm(ks[15], (DEPTH, D_MODEL, 2 * D_FF), D_MODEL ** -0.5),
        'conv_w': nrm(ks[16], (DEPTH, CONV_W, 2 * D_FF), CONV_W ** -0.5),
        'conv_b': nrm(ks[17], (DEPTH, 2 * D_FF), 0.01),
        'w_down': nrm(ks[18], (DEPTH, D_FF, D_MODEL), D_FF ** -0.5),
        'norm_final_g': gain(ks[19], (D_MODEL,)),
    }


def reference(x_prompt, x_sample, norm_mix_g, w_in, lam_q1, lam_k1, lam_q2, lam_k2, subln_g,
              q_norm_g, k_norm_g, w_proj_a, w_proj_b, w_out, norm_ffn_g, w_up, conv_w, conv_b,
              w_down, norm_final_g):
    y_prompt = _trunk(x_prompt, norm_mix_g, w_in, lam_q1, lam_k1, lam_q2, lam_k2, subln_g, q_norm_g,
                      k_norm_g, w_proj_a, w_proj_b, w_out, norm_ffn_g, w_up, conv_w, conv_b, w_down,
                      norm_final_g)
    y_sample = _trunk(x_sample, norm_mix_g, w_in, lam_q1, lam_k1, lam_q2, lam_k2, subln_g, q_norm_g,
                      k_norm_g, w_proj_a, w_proj_b, w_out, norm_ffn_g, w_up, conv_w, conv_b, w_down,
                      norm_final_g)
    return (y_prompt, y_sample)
```

```python
import contextlib
import numpy as np
import concourse.bass as bass
import concourse.mybir as mybir
from concourse.bass_utils import run_bass_kernel_spmd

F32 = mybir.dt.float32
BF16 = mybir.dt.bfloat16
AF = mybir.ActivationFunctionType
ALU = mybir.AluOpType
AX = mybir.AxisListType

D = 1024
DC = 8
DFF = 2816
NFF = 22
D_IN = 6656
EPS = 1e-6
GRID_W = 64
N_CORES = 8
ENGS = ("pe", "act", "dve", "pool", "sp")
SEM_EPOCH = 24000
DUMMY_MM = 0

OQ, OG, OK_, OV = 0, 2048, 4096, 5376


class _StopBuild(Exception):
    pass


class Op:
    __slots__ = ("eng", "fn", "deps", "key", "inc", "needs_inc", "epoch", "value", "idx")


class Prog:
    def __init__(self, nc):
        self.nc = nc
        self.q = {e: [] for e in ENGS}
        self.res_w = {}
        self.res_r = {}
        self.last_by_key = {}
        self.pending_barrier = {}
        self.n_ops = 0
        self.limit = None

    def _record(self, eng, fn, reads, writes, key=None, inc=1):
        op = Op()
        op.eng = eng
        op.fn = fn
        is_dma = key is not None
        op.key = key if is_dma else eng
        op.inc = inc
        op.needs_inc = is_dma
        op.epoch = 0
        op.value = 0
        op.idx = self.n_ops
        self.n_ops += 1
        deps = {}
        me = (eng, key) if is_dma else eng
        for r in reads:
            w = self.res_w.get(r)
            if w:
                for o in w.values():
                    deps[id(o)] = o
        for r in writes:
            rd = self.res_r.get(r)
            if rd:
                for e, o in rd.items():
                    deps[id(o)] = o
            w = self.res_w.get(r)
            if w:
                for e, o in w.items():
                    if e != me or is_dma or eng != "pe":
                        deps[id(o)] = o
        pb = self.pending_barrier.pop(eng, None)
        if pb:
            for o in pb:
                deps[id(o)] = o
        op.deps = list(deps.values())
        for o in op.deps:
            o.needs_inc = True
        for r in reads:
            self.res_r.setdefault(r, {})[me] = op
        for r in writes:
            if self.res_r.get(r):
                self.res_w[r] = {me: op}
                self.res_r[r] = {}
            else:
                self.res_w.setdefault(r, {})[me] = op
        self.q[eng].append(op)
        self.last_by_key[op.key] = op
        return op

    def op(self, eng, fn, reads=(), writes=()):
        return self._record(eng, fn, reads, writes)

    def dma(self, eng, key, out, in_, reads=(), writes=()):
        return self._record(eng, lambda e: e.dma_start(out=out, in_=in_), reads, writes,
                            key=("dma", key), inc=16)

    def barrier(self):
        lasts = list(self.last_by_key.values())
        for e in ENGS:
            self.pending_barrier[e] = list(lasts)

    def emit(self):
        nc = self.nc
        if self.limit is not None:
            for e in ENGS:
                self.q[e] = [o for o in self.q[e] if o.idx < self.limit]
        print("n_ops", self.n_ops, {e: len(self.q[e]) for e in ENGS})
        keycount = {}
        sems = {}
        for e in ENGS:
            for op in self.q[e]:
                if not op.needs_inc:
                    continue
                ep, cnt = keycount.get(op.key, (0, 0))
                if cnt + op.inc > SEM_EPOCH:
                    ep, cnt = ep + 1, 0
                cnt += op.inc
                keycount[op.key] = (ep, cnt)
                op.epoch, op.value = ep, cnt
                sems[(op.key, ep)] = None
        last = {}
        for e in ENGS:
            for op in self.q[e]:
                if op.needs_inc:
                    k = (op.key, op.epoch)
                    last[k] = max(last.get(k, 0), op.value)
        with contextlib.ExitStack() as st:
            for i, k in enumerate(list(sems.keys())):
                sems[k] = st.enter_context(nc.semaphore(f"s{i}"))
            block = st.enter_context(nc.Block())

            def run(name, eng):
                waited = {}
                for op in self.q[name]:
                    need = {}
                    for d in op.deps:
                        k = (d.key, d.epoch)
                        if waited.get(k, 0) >= d.value:
                            continue
                        if need.get(k, 0) < d.value:
                            need[k] = d.value
                    for k, v in need.items():
                        eng.wait_ge(sems[k], v)
                        waited[k] = v
                    ins = op.fn(eng)
                    if op.needs_inc:
                        ins.then_inc(sems[(op.key, op.epoch)], op.inc)
                if name == "sp":
                    for k, v in last.items():
                        if waited.get(k, 0) < v:
                            eng.wait_ge(sems[k], v)

            @block.tensor
            def _(eng):
                run("pe", eng)

            @block.scalar
            def _(eng):
                run("act", eng)

            @block.vector
            def _(eng):
                run("dve", eng)

            @block.gpsimd
            def _(eng):
                run("pool", eng)

            @block.sync
            def _(eng):
                run("sp", eng)


PP_SUBLN, PP_QN, PP_KN, PP_EPS, PP_LAM = 0, 1, 2, 3, 4
PP_CW = 4 + 256
PP_CB = PP_CW + 2 * 3 * 44
NPP = PP_CB + 44
CM_ID, CM_RA, CM_RB, CM_S0, CM_S1, CM_BLK, CM_ONE, CM_SELB = range(8)
NCM = 8


def build_program(S, jobs, revs=(False, False), stop=None):
    NT = S // 512
    KC = S // 128
    SEG = min(1024, S)
    NSEG = S // SEG
    SEGC = SEG // 128
    nc = bass.Bass("TRN2", target_bir_lowering=False)
    nj = len(jobs)

    def dram(name, shape, dt, kind):
        return nc.dram_tensor(name, shape, dt, kind=kind).ap()

    xs = [dram(f"x{j}", [S, D], F32, "ExternalInput") for j in range(nj)]
    tabs = [dram(f"tab{j}", [4, 128, S], F32, "ExternalInput") for j in range(nj)]
    ys = [dram(f"y{j}", [jobs[j][1], D], F32, "ExternalOutput") for j in range(nj)]
    w_in = dram("w_in", [D, D_IN], F32, "ExternalInput")
    w_pa = dram("w_pa", [D, D], F32, "ExternalInput")
    w_pb = dram("w_pb", [D, D], F32, "ExternalInput")
    w_o = dram("w_o", [D, D], F32, "ExternalInput")
    w_up = dram("w_up", [D, 2 * DFF], F32, "ExternalInput")
    w_dn = dram("w_dn", [DFF, D], F32, "ExternalInput")
    gvec = dram("gvec", [3, 128, D], F32, "ExternalInput")
    ppd = dram("pp", [128, NPP], F32, "ExternalInput")
    cmd = dram("cm", [128, NCM, 128], F32, "ExternalInput")
    b_in = dram("b_in", [D, D_IN], BF16, "Internal")
    b_pa = dram("b_pa", [D, D], BF16, "Internal")
    b_pb = dram("b_pb", [D, D], BF16, "Internal")
    b_o = dram("b_o", [D, D], BF16, "Internal")
    b_up = dram("b_up", [D, 2 * DFF], BF16, "Internal")
    b_dn = dram("b_dn", [DFF, D], BF16, "Internal")
    KT = dram("KT", [10, 128, S], BF16, "Internal")
    VA = dram("VA", [8, S, 128], BF16, "Internal")
    VG = dram("VG", [4, S, 64], BF16, "Internal")
    X1 = dram("X1", [S, D], F32, "Internal")

    P = Prog(nc)
    st = contextlib.ExitStack()
    with st:
        def sb(name, shape, dt):
            return st.enter_context(nc.sbuf_tensor("s_" + name, shape, dt))

        pp = sb("pp", [128, NPP], F32)
        cmf = sb("cmf", [128, NCM, 128], F32)
        cmb = sb("cmb", [128, NCM, 128], BF16)
        gv = sb("gv", [128, 3, D], F32)
        lamt = sb("lamt", [128, 8], F32)
        mk = sb("mk", [128, 32], F32)
        mke = sb("mke", [128, 32], F32)
        mq = sb("mq", [128, 32], F32)
        negc = sb("negc", [128, 32], F32)
        small = sb("small", [128, 16], F32)
        mk_tmp = sb("mk_tmp", [128, 8], F32)
        xt = sb("xt", [128, 4, D], F32)
        hb = sb("hb", [128, 4, D], BF16)
        hT = sb("hT", [128, DC, 512], BF16)
        wbuf = [sb(f"wbuf{i}", [128, 4096], BF16) for i in range(3)]
        tb = sb("tb", [128, 4, 512], F32)
        NSET = 3
        tmpf = [sb(f"tmpf{i}", [128, 512], F32) for i in range(4 * NSET)]
        tmpb = [sb(f"tmpb{i}", [128, 512], BF16) for i in range(3 * NSET)]
        rot = {"i": 0}
        SCR = [(0, 1), (2, 3), (6, 7)]
        ps = st.enter_context(nc.psum_tensor("ps", [128, 4096], F32))

        def bank(i, n=512, parts=128):
            return ps[0:parts, i * 512:i * 512 + n]

        BK = [f"B{i}" for i in range(8)]
        ident = cmb[:, CM_ID, :]

        def MM(out, lhsT, rhs, start, stop, r, w):
            P.op("pe", lambda e: e.matmul(out, lhsT=lhsT, rhs=rhs, start=start, stop=stop), r, w)

        def TR(out, in_, r, w):
            P.op("pe", lambda e: e.transpose(out=out, in_=in_, identity=ident), r, w)

        def ACT(out, in_, func, r, w, scale=1.0, bias=None, accum=None):
            kw = {}
            if bias is not None:
                kw["bias"] = bias
            if accum is not None:
                kw["accum_out"] = accum
            P.op("act", lambda e: e.activation(out=out, in_=in_, func=func, scale=scale, **kw), r, w)

        def TT(eng, out, in0, in1, op, r, w):
            P.op(eng, lambda e: e.tensor_tensor(out=out, in0=in0, in1=in1, op=op), r, w)

        def STT(eng, out, in0, scalar, in1, op0, op1, r, w):
            P.op(eng, lambda e: e.scalar_tensor_tensor(out=out, in0=in0, scalar=scalar, in1=in1,
                                                       op0=op0, op1=op1), r, w)

        def TS(eng, out, in0, s1, s2, op0, op1, r, w):
            P.op(eng, lambda e: e.tensor_scalar(out=out, in0=in0, scalar1=s1, scalar2=s2,
                                                op0=op0, op1=op1), r, w)

        def TS1(eng, out, in0, s1, op, r, w):
            P.op(eng, lambda e: e.tensor_single_scalar(out=out, in_=in0, scalar=s1, op=op), r, w)

        def CP(eng, out, in_, r, w):
            if eng == "act":
                P.op("act", lambda e: e.copy(out=out, in_=in_), r, w)
            else:
                P.op(eng, lambda e: e.tensor_copy(out=out, in_=in_), r, w)

        def MEMSET(eng, ap, val, w):
            P.op(eng, lambda e: e.memset(ap, val), (), w)

        def RMAX(out, in_, r, w):
            P.op("dve", lambda e: e.reduce_max(out=out, in_=in_, axis=AX.X), r, w)

        def RSUM(out, in_, r, w):
            P.op("dve", lambda e: e.reduce_sum(out=out, in_=in_, axis=AX.X), r, w)

        def RECIP(out, in_, r, w):
            P.op("dve", lambda e: e.reciprocal(out=out, in_=in_), r, w)

        dmaq = ["sp", "pool"]

        P.dma("sp", "c0", pp[:], ppd[:, :], writes=["pp"])
        P.dma("sp", "c1", cmf[:], cmd[:, :, :], writes=["cmf"])
        P.dma("sp", "c2", gv[:], gvec.rearrange("g p d -> p g d"), writes=["gv"])
        CP("dve", cmb[:], cmf[:], ["cmf"], ["cmb"])
        for i in range(2):
            TT("dve", tmpf[0][:, 0:64], pp[:, PP_LAM + 128 * i:PP_LAM + 128 * i + 64],
               pp[:, PP_LAM + 128 * i + 64:PP_LAM + 128 * i + 128], ALU.mult, ["pp"], ["tmpf0"])
            RSUM(lamt[:, i:i + 1], tmpf[0][:, 0:64], ["tmpf0"], ["lamt"])
        ACT(lamt[:, 2:4], lamt[:, 0:2], AF.Exp, ["lamt"], ["lamt"])
        TT("dve", lamt[:, 4:5], lamt[:, 3:4], lamt[:, 2:3], ALU.subtract, ["lamt"], ["lamt"])
        TS1("dve", lamt[:, 5:6], lamt[:, 4:5], -0.2, ALU.add, ["lamt"], ["lamt"])
        TS1("dve", lamt[:, 6:7], pp[:, PP_SUBLN:PP_SUBLN + 1], 0.8, ALU.mult, ["pp", "lamt"], ["lamt"])
        neglam = lamt[:, 5:6]
        gsub = lamt[:, 6:7]
        eps_ap = pp[:, PP_EPS:PP_EPS + 1]

        wi = 0
        for (src, dst, rows, cols) in ((w_in, b_in, D, D_IN), (w_pa, b_pa, D, D), (w_pb, b_pb, D, D),
                                       (w_o, b_o, D, D), (w_up, b_up, D, 2 * DFF), (w_dn, b_dn, DFF, D)):
            for r0 in range(0, rows, 128):
                for c0 in range(0, cols, 4096):
                    cw = min(4096, cols - c0)
                    slot = wi % 3
                    P.dma("pool", f"wc{slot}", wbuf[slot][:, 0:cw], src[r0:r0 + 128, c0:c0 + cw],
                          writes=[f"wbuf{slot}"])
                    P.dma("sp", f"ws{slot}", dst[r0:r0 + 128, c0:c0 + cw], wbuf[slot][:, 0:cw],
                          reads=[f"wbuf{slot}"], writes=["wscratch"])
                    wi += 1
        P.barrier()
        if stop == 0:
            jobs = []

        wstate = {"i": 0}

        def load_piece(src_ap, view):
            slot = wstate["i"] % 3
            wstate["i"] += 1
            dst = view(wbuf[slot])
            P.dma("sp", f"wl{slot}", dst, src_ap, reads=["wscratch"], writes=[f"wbuf{slot}"])
            return dst, f"wbuf{slot}"

        def wview_d(cols):
            return lambda t: t[:, 0:DC * cols].rearrange("p (c n) -> p c n", c=DC)

        def piece_in(c0, cols):
            return load_piece(b_in[:, c0:c0 + cols].rearrange("(c p) n -> p c n", p=128), wview_d(cols))

        def stage_a(src, row0, c_lo, c_hi, gi, zero_first):
            if zero_first:
                MEMSET("pool", xt[:], 0.0, ["xt"])
            for b in range(4):
                lo, hi = max(c_lo, 128 * b), min(c_hi, 128 * b + 128)
                if lo >= hi:
                    continue
                P.dma("sp", "xl", xt[lo - 128 * b:hi - 128 * b, b, :],
                      src[row0 + lo - c_lo:row0 + hi - c_lo, :], writes=["xt"])
            norm_tm(xt, "xt", gi, hb, "hb")
            nb = (c_hi + 127) // 128
            psT = ps[:, 0:2048].bitcast(BF16).rearrange("p (c n) -> p c n", c=DC)
            for b in range(nb):
                for c in range(DC):
                    TR(psT[:, c, b * 128:(b + 1) * 128], hb[:, b, c * 128:(c + 1) * 128],
                       ["hb", "cmb"], [BK[c // 2]])
            w = nb * 128
            for c2 in range(4):
                eng = "dve" if c2 % 2 == 0 else "act"
                CP(eng, hT[:, 2 * c2:2 * c2 + 2, 0:w], psT[:, 2 * c2:2 * c2 + 2, 0:w], [BK[c2]], ["hT"])

        def norm_tm(src_t, src_key, gi, out_t, out_key):
            for b in range(4):
                ACT(hb[:, b, :], src_t[:, b, :], AF.Square, [src_key], ["hb", "ss"], accum=small[:, b:b + 1])
            ACT(small[:, 4:8], small[:, 0:4], AF.Ln, ["ss"], ["ss2"], scale=1.0 / D, bias=eps_ap)
            ACT(small[:, 8:12], small[:, 4:8], AF.Exp, ["ss2"], ["rstd"], scale=-0.5)
            for b in range(4):
                STT("dve", out_t[:, b, :], src_t[:, b, :], small[:, 8 + b:9 + b],
                    gv[:, gi, :], ALU.mult, ALU.mult, [src_key, "rstd", "gv", "hb"], [out_key])

        def finish_qk(psq_bank, is_b, gcol, scale, out_ap, out_key, bound_dst, bound_col, running):
            st_ = rot["i"] % NSET
            rot["i"] += 1
            tf = [tmpf[4 * st_ + i] for i in range(4)]
            tfk = [f"tmpf{4 * st_ + i}" for i in range(4)]
            tbb = [tmpb[3 * st_ + i] for i in range(3)]
            tbk = [f"tmpb{3 * st_ + i}" for i in range(3)]
            b0, b1 = SCR[st_]
            if is_b:
                ACT(tbb[0][:], bank(psq_bank), AF.Square, [BK[psq_bank]], [tbk[0]])
                CP("dve", tf[0][:], bank(psq_bank), [BK[psq_bank], tbk[0]], [tfk[0]])
                MM(bank(b0), cmb[:, CM_BLK, :], tbb[0][:], True, True, [tbk[0], "cmb"], [BK[b0]])
                ACT(tf[1][:], bank(b0), AF.Ln, [BK[b0]], [tfk[1]], scale=1.0 / 64, bias=eps_ap)
                ACT(tf[1][:], tf[1][:], AF.Exp, [tfk[1]], [tfk[1]], scale=-0.5)
                STT("dve", tbb[1][:], tf[0][:], pp[:, gcol:gcol + 1], tf[1][:], ALU.mult, ALU.mult,
                    [tfk[0], tfk[1], "pp"], [tbk[1]])
                rm, ci, si = CM_RB, 2, 3
            else:
                CP("act", tbb[1][:], bank(psq_bank), [BK[psq_bank]], [tbk[1]])
                rm, ci, si = CM_RA, 0, 1
            MM(bank(b1), cmb[:, rm, :], tbb[1][:], True, True, [tbk[1], "cmb"], [BK[b1]])
            if scale == 1.0:
                TT("pool", tf[2][:], tbb[1][:], tb[:, ci, :], ALU.mult, [tbk[1], "tb"], [tfk[2]])
                TT("dve", tf[3][:], bank(b1), tb[:, si, :], ALU.mult, [BK[b1], "tb"], [tfk[3]])
            else:
                STT("dve", tf[2][:], tbb[1][:], scale, tb[:, ci, :], ALU.mult, ALU.mult, [tbk[1], "tb"], [tfk[2]])
                STT("dve", tf[3][:], bank(b1), scale, tb[:, si, :], ALU.mult, ALU.mult, [BK[b1], "tb"], [tfk[3]])
            TT("pool", out_ap, tf[2][:], tf[3][:], ALU.add, [tfk[2], tfk[3]], [out_key])
            TT("pool", tbb[2][:], out_ap, out_ap, ALU.mult, [out_key], [tbk[2]])
            for h in range(2):
                bb = (b0, b1)[h]
                MM(bank(bb), cmb[:, CM_S0 + h, :], tbb[2][:], True, True, [tbk[2], "cmb"], [BK[bb]])
                if running:
                    sc = small[:, 12 + 2 * st_ + h - 0:13 + 2 * st_ + h - 0] if False else mk_tmp[:, 2 * st_ + h:2 * st_ + h + 1]
                    RMAX(sc, bank(bb), [BK[bb]], [("mkt", st_, h)])
                    TT("dve", bound_dst[:, bound_col + h:bound_col + h + 1], bound_dst[:, bound_col + h:bound_col + h + 1],
                       sc, ALU.max, [("mkt", st_, h), ("mkq", bound_col + h)], [("mkq", bound_col + h)])
                else:
                    RMAX(bound_dst[:, bound_col + h:bound_col + h + 1], bank(bb), [BK[bb]], [("mkq", bound_col + h)])

        try:
            for j, (nq, nout) in enumerate(jobs):
                x, tab, y = xs[j], tabs[j], ys[j]
                with contextlib.ExitStack() as p1:
                    wk = p1.enter_context(nc.sbuf_tensor(f"wk{j}", [128, DC, 1280], BF16))
                    wv = p1.enter_context(nc.sbuf_tensor(f"wv{j}", [128, DC, 1280], BF16))
                    kout = [p1.enter_context(nc.sbuf_tensor(f"kout{j}_{i}", [128, 512], BF16)) for i in range(3)]
                    vsb = [p1.enter_context(nc.sbuf_tensor(f"vsb{j}_{i}", [128, 1280], BF16)) for i in range(2)]
                    P.dma("sp", "wk", wk[:], b_in[:, OK_:OK_ + 1280].rearrange("(c p) n -> p c n", p=128),
                          reads=["wscratch"], writes=["wk"])
                    P.dma("sp", "wv", wv[:], b_in[:, OV:OV + 1280].rearrange("(c p) n -> p c n", p=128),
                          reads=["wscratch"], writes=["wv"])
                    MEMSET("dve", mk[:], 0.0, [("mkq", c_) for c_ in range(32)])
                    for t in range(NT):
                        s = t * 512
                        stage_a(x, s, 0, 512, 0, False)
                        if stop == 10:
                            raise _StopBuild()
                        P.dma("sp", "tb", tb[:], tab[:, :, s:s + 512].rearrange("f p n -> p f n"), writes=["tb"])
                        for kc in range(10):
                            if stop == 11 and kc == 1:
                                raise _StopBuild()
                            if stop == 12 and kc == 9:
                                raise _StopBuild()
                            pb_ = 4 + (kc % 2)
                            for dc in range(DC):
                                MM(bank(pb_), wk[:, dc, kc * 128:(kc + 1) * 128], hT[:, dc, :], dc == 0, dc == DC - 1,
                                   ["wk", "hT"], [BK[pb_]])
                            ko = kout[kc % 3]
                            kkey = f"kout{kc % 3}"
                            finish_qk(pb_, kc >= 8, PP_KN, 1.0, ko[:], kkey, mk, 2 * kc, True)
                            P.dma("pool", f"kst{kc % 3}", KT[kc, :, s:s + 512], ko[:], reads=[kkey],
                                  writes=[("KT", kc, s // SEG)])
                        if stop == 13:
                            raise _StopBuild()
                        for b in range(4):
                            vs = vsb[b % 2]
                            vkey = f"vsb{b % 2}"
                            for pi, (c0, cw) in enumerate(((0, 512), (512, 512), (1024, 256))):
                                for dc in range(DC):
                                    MM(bank(pi, cw), hT[:, dc, b * 128:(b + 1) * 128], wv[:, dc, c0:c0 + cw],
                                       dc == 0, dc == DC - 1, ["wv", "hT"], [BK[pi]])
                                CP("act" if pi == 1 else "dve", vs[:, c0:c0 + cw], bank(pi, cw), [BK[pi]], [vkey])
                            r0 = s + b * 128
                            P.dma("pool", f"vst{b % 2}", VA[:, r0:r0 + 128, :].rearrange("h p d -> p h d"),
                                  vs[:, 0:1024].rearrange("p (h d) -> p h d", h=8), reads=[vkey],
                                  writes=[("VA", r0 // SEG)])
                            P.dma("pool", f"vsg{b % 2}", VG[:, r0:r0 + 128, :].rearrange("h p d -> p h d"),
                                  vs[:, 1024:1280].rearrange("p (h d) -> p h d", h=4), reads=[vkey],
                                  writes=[("VG", r0 // SEG)])
                    CP("dve", mke[:, 0:16], mk[:, 0:16], [("mkq", c_) for c_ in range(32)], ["mke"])
                    for jj in range(8):
                        CP("dve", mke[:, 16 + 2 * jj:18 + 2 * jj], mk[:, 16 + 2 * (jj // 4):18 + 2 * (jj // 4)],
                           [("mkq", c_) for c_ in range(32)], ["mke"])
                    P.barrier()
                if stop == 1:
                    raise _StopBuild()

                with contextlib.ExitStack() as p2:
                    def sb2(name, shape, dt):
                        return p2.enter_context(nc.sbuf_tensor(f"{name}{j}", shape, dt))
                    qT = sb2("qT", [128, 16, 512], BF16)
                    kseg = [sb2(f"kseg{i}", [128, SEG], BF16) for i in range(3)]
                    vseg = [sb2(f"vseg{i}", [128, SEGC, 128], BF16) for i in range(3)]
                    vsgg = [sb2(f"vsgg{i}", [128, SEGC, 64], BF16) for i in range(3)]
                    pT = [sb2(f"pT{i}", [128, 1024], BF16) for i in range(3)]
                    osb = [sb2(f"osb{i}", [128, 512], F32) for i in range(2)]
                    oaT = sb2("oaT", [128, 8, 512], BF16)
                    obT = sb2("obT", [64, 16, 512], BF16)
                    mixT = hb[:].rearrange("p b d -> p (b d)").rearrange("p (c n) -> p c n", c=8)
                    x1t = sb2("x1t", [128, 2, D], F32)

                    for t in range(nq):
                        s = t * 512
                        stage_a(x, s, 0, 512, 0, False)
                        P.dma("sp", "tb", tb[:], tab[:, :, s:s + 512].rearrange("f p n -> p f n"), writes=["tb"])
                        for pc in range(4):
                            wp, wkey = piece_in(OQ + pc * 512, 512)
                            for cc in range(4):
                                c = pc * 4 + cc
                                pb_ = 4 + (c % 2)
                                for dc in range(DC):
                                    MM(bank(pb_), wp[:, dc, cc * 128:(cc + 1) * 128], hT[:, dc, :], dc == 0, dc == DC - 1,
                                       [wkey, "hT"], [BK[pb_]])
                                finish_qk(pb_, c >= 8, PP_QN, 0.125, qT[:, c, :], ("qT", c), mq, 2 * c, False)
                        TT("dve", negc[:], mq[:], mke[:], ALU.mult, [("mkq", c_) for c_ in range(32)] + ["mke"], ["negc"])
                        ACT(negc[:], negc[:], AF.Ln, ["negc"], ["negc"])
                        ACT(negc[:], negc[:], AF.Exp, ["negc"], ["negc"], scale=0.5)
                        TS1("dve", negc[:], negc[:], -1.0, ALU.mult, ["negc"], ["negc"])

                        units = []
                        for h in range(8):
                            units.append(dict(kind="A", kc=h, rows=[(0, 64), (64, 128)], qc=[h, h], vh=h,
                                              ci=[2 * h, 2 * h + 1], out=h))
                        for n in range(4):
                            for pr in range(2):
                                r0 = 64 * (n % 2)
                                qcs = [8 + (n // 2) * 4 + 2 * pr, 8 + (n // 2) * 4 + 2 * pr + 1]
                                units.append(dict(kind="B", kc=8 + n // 2, rows=[(r0, r0 + 64)] * 2, qc=qcs, vh=n,
                                                  ci=[2 * qcs[0] + n % 2, 2 * qcs[1] + n % 2],
                                                  out=[4 * n + 2 * pr, 4 * n + 2 * pr + 1]))
                        segs = [(u, sg) for u in range(len(units)) for sg in range(NSEG)]

                        def load_seg(i):
                            u, sg = segs[i]
                            un = units[u]
                            sl = i % 3
                            P.dma("sp", f"ks{sl}", kseg[sl][:], KT[un["kc"], :, sg * SEG:(sg + 1) * SEG],
                                  reads=[("KT", un["kc"], sg)], writes=[f"kseg{sl}"])
                            if un["kind"] == "A":
                                P.dma("sp", f"vs{sl}", vseg[sl][:],
                                      VA[un["vh"], sg * SEG:(sg + 1) * SEG, :].rearrange("(c p) d -> p c d", p=128),
                                      reads=[("VA", sg)], writes=[f"vseg{sl}"])
                            else:
                                P.dma("sp", f"vs{sl}", vsgg[sl][:],
                                      VG[un["vh"], sg * SEG:(sg + 1) * SEG, :].rearrange("(c p) d -> p c d", p=128),
                                      reads=[("VG", sg)], writes=[f"vsgg{sl}"])

                        steps = []
                        for i, (u, sg) in enumerate(segs):
                            for sj in range(2):
                                for g in range(SEGC // 2):
                                    steps.append((i, u, sg, sj, g))
                        nsteps = len(steps)

                        def rec_qk(k):
                            i, u, sg, sj, g = steps[k]
                            un = units[u]
                            sl = i % 3
                            r0, r1 = un["rows"][sj]
                            stb = k % 2
                            for kk in range(2):
                                col = (2 * g + kk) * 128
                                MM(bank(2 * stb + kk), kseg[sl][r0:r1, col:col + 128], qT[r0:r1, un["qc"][sj], :],
                                   True, True, [f"kseg{sl}", ("qT", un["qc"][sj])], [BK[2 * stb + kk]])

                        def rec_rest(k):
                            i, u, sg, sj, g = steps[k]
                            un = units[u]
                            sl = i % 3
                            stb = k % 2
                            pt = pT[k % 3]
                            ptk = f"pT{k % 3}"
                            ci = un["ci"][sj]
                            ACT(pt[:], ps[:, stb * 1024:(stb + 1) * 1024], AF.Exp, [BK[2 * stb], BK[2 * stb + 1], "negc"], [ptk],
                                bias=negc[:, ci:ci + 1])
                            gi_ = sg * (SEGC // 2) + g
                            first = gi_ == 0
                            last = gi_ == NSEG * (SEGC // 2) - 1
                            isA = un["kind"] == "A"
                            for kk in range(2):
                                c = 2 * g + kk
                                if isA:
                                    lhsT = vseg[sl][:, c, :]
                                    out = bank(4 + sj)
                                    vk = f"vseg{sl}"
                                else:
                                    lhsT = vsgg[sl][:, c, :]
                                    out = bank(4 + sj, 512, 64)
                                    vk = f"vsgg{sl}"
                                MM(out, lhsT, pt[:, kk * 512:(kk + 1) * 512], first and kk == 0, last and kk == 1,
                                   [vk, ptk], [BK[4 + sj]])
                            for kk in range(2):
                                MM(bank(6 + sj), cmb[:, CM_ONE, :], pt[:, kk * 512:(kk + 1) * 512],
                                   first and kk == 0, last and kk == 1, ["cmb", ptk], [BK[6 + sj]])
                            if last:
                                finalize(u, sj)

                        def finalize(u, sj):
                            un = units[u]
                            st_ = rot["i"] % NSET
                            rot["i"] += 1
                            tf = [tmpf[4 * st_ + i] for i in range(4)]
                            tfk = [f"tmpf{4 * st_ + i}" for i in range(4)]
                            if un["kind"] == "A":
                                h = un["out"]
                                RECIP(tf[0][:], bank(6 + sj), [BK[6 + sj]], [tfk[0]])
                                TT("dve", osb[sj][:], bank(4 + sj), tf[0][:], ALU.mult, [BK[4 + sj], tfk[0]], [f"osb{sj}"])
                                if sj == 1:
                                    STT("dve", tf[1][:], osb[1][:], neglam, osb[0][:], ALU.mult, ALU.add,
                                        ["osb0", "osb1", "lamt"], [tfk[1]])
                                    tbs = tmpb[3 * st_]
                                    tbsk = f"tmpb{3 * st_}"
                                    TT("pool", tbs[:], tf[1][:], tf[1][:], ALU.mult, [tfk[1]], [tbsk])
                                    MM(bank(6), cmb[:, CM_ONE, :], tbs[:], True, True, [tbsk, "cmb"], [BK[6]])
                                    ACT(tf[2][:], bank(6), AF.Ln, [BK[6]], [tfk[2]], scale=1.0 / 128, bias=eps_ap)
                                    ACT(tf[2][:], tf[2][:], AF.Exp, [tfk[2]], [tfk[2]], scale=-0.5)
                                    STT("dve", oaT[:, h, :], tf[1][:], gsub, tf[2][:], ALU.mult, ALU.mult,
                                        [tfk[1], tfk[2], "lamt"], [("oaT", h)])
                            else:
                                hb_ = un["out"][sj]
                                RECIP(tf[0][0:64, :], bank(6 + sj, 512, 64), [BK[6 + sj]], [tfk[0]])
                                TT("dve", obT[:, hb_, :], bank(4 + sj, 512, 64), tf[0][0:64, :], ALU.mult,
                                   [BK[4 + sj], tfk[0]], [("obT", hb_)])

                        load_seg(0)
                        if len(segs) > 1:
                            load_seg(1)
                        rec_qk(0)
                        for k in range(nsteps):
                            i, u, sg, sj, g = steps[k]
                            if sj == 0 and g == 0 and i + 2 < len(segs):
                                load_seg(i + 2)
                            if k + 1 < nsteps:
                                rec_qk(k + 1)
                            rec_rest(k)

                        for m in range(8):
                            slot = wstate["i"] % 3
                            wstate["i"] += 1
                            tA = wbuf[slot]
                            wkA = f"wbuf{slot}"
                            wa = tA[:, 0:1024].rearrange("p (c n) -> p c n", c=DC)
                            wga = tA[:, 1024:2048].rearrange("p (c n) -> p c n", c=DC)
                            wgb = tA[:, 2048:3072].rearrange("p (c n) -> p c n", c=DC)
                            P.dma("sp", f"wl{slot}", wa, b_pa[:, m * 128:(m + 1) * 128].rearrange("(c p) n -> p c n", p=128),
                                  reads=["wscratch"], writes=[wkA])
                            P.dma("sp", f"wl{slot}", wga,
                                  b_in[:, OG + m * 128:OG + (m + 1) * 128].rearrange("(c p) n -> p c n", p=128),
                                  reads=["wscratch"], writes=[wkA])
                            P.dma("sp", f"wl{slot}", wgb,
                                  b_in[:, OG + 1024 + m * 128:OG + 1024 + (m + 1) * 128].rearrange("(c p) n -> p c n", p=128),
                                  reads=["wscratch"], writes=[wkA])
                            slot = wstate["i"] % 3
                            wstate["i"] += 1
                            wkB = f"wbuf{slot}"
                            wb_ = wbuf[slot][0:64, 0:2048].rearrange("p (h n) -> p h n", h=16)
                            P.dma("sp", f"wl{slot}", wb_, b_pb[:, m * 128:(m + 1) * 128].rearrange("(h p) n -> p h n", p=64),
                                  reads=["wscratch"], writes=[wkB])
                            st_ = m % NSET
                            tf = [tmpf[4 * st_ + i] for i in range(4)]
                            tfk = [f"tmpf{4 * st_ + i}" for i in range(4)]
                            q0 = 4 * (m % 2)
                            for dc in range(DC):
                                MM(bank(q0), wa[:, dc, :], oaT[:, dc, :], dc == 0, dc == DC - 1,
                                   [wkA, ("oaT", dc)], [BK[q0]])
                            for hh in range(16):
                                MM(bank(q0 + 1), wb_[:, hh, :], obT[:, hh, :], hh == 0, hh == 15,
                                   [wkB, ("obT", hh)], [BK[q0 + 1]])
                            for dc in range(DC):
                                MM(bank(q0 + 2), wga[:, dc, :], hT[:, dc, :], dc == 0, dc == DC - 1, [wkA, "hT"], [BK[q0 + 2]])
                            for dc in range(DC):
                                MM(bank(q0 + 3), wgb[:, dc, :], hT[:, dc, :], dc == 0, dc == DC - 1, [wkA, "hT"], [BK[q0 + 3]])
                            ACT(tf[0][:], bank(q0 + 2), AF.Tanh, [BK[q0 + 2]], [tfk[0]], scale=0.5)
                            ACT(tf[1][:], bank(q0 + 3), AF.Tanh, [BK[q0 + 3]], [tfk[1]], scale=0.5)
                            TS("pool", tf[0][:], tf[0][:], 0.5, 0.5, ALU.mult, ALU.add, [tfk[0]], [tfk[0]])
                            TS("pool", tf[1][:], tf[1][:], 0.5, 0.5, ALU.mult, ALU.add, [tfk[1]], [tfk[1]])
                            TT("dve", tf[2][:], tf[0][:], bank(q0), ALU.mult, [tfk[0], BK[q0]], [tfk[2]])
                            TT("dve", tf[3][:], tf[1][:], bank(q0 + 1), ALU.mult, [tfk[1], BK[q0 + 1]], [tfk[3]])
                            TT("pool", mixT[:, m, :], tf[2][:], tf[3][:], ALU.add, [tfk[2], tfk[3]], ["hb"])
                        wos = []
                        for nh in range(2):
                            wos.append(load_piece(b_o[:, nh * 512:(nh + 1) * 512].rearrange("(c p) n -> p c n", p=128),
                                                  wview_d(512)))
                        for b in range(4):
                            xb_ = b % 2
                            for nh in range(2):
                                wo, wok = wos[nh]
                                pb_ = 4 + 2 * xb_ + nh
                                for dc in range(DC):
                                    MM(bank(pb_), mixT[:, dc, b * 128:(b + 1) * 128], wo[:, dc, :], dc == 0, dc == DC - 1,
                                       [wok, "hb"], [BK[pb_]])
                                TT("dve", x1t[:, xb_, nh * 512:(nh + 1) * 512], bank(pb_), xt[:, b, nh * 512:(nh + 1) * 512],
                                   ALU.add, [BK[pb_], "xt"], [("x1t", xb_)])
                            P.dma("pool", f"x1s{xb_}", X1[s + b * 128:s + (b + 1) * 128, :], x1t[:, xb_, :],
                                  reads=[("x1t", xb_)], writes=[("X1", t)])
                    P.barrier()
                if stop == 2:
                    raise _StopBuild()

                with contextlib.ExitStack() as p3:
                    def sb3(name, shape, dt):
                        return p3.enter_context(nc.sbuf_tensor(f"{name}{j}", shape, dt))
                    aT = sb3("aT", [128, NFF, 512], BF16)
                    xres = sb3("xres", [128, 4, D], F32)
                    yt = sb3("yt", [128, 4, D], F32)
                    yo = sb3("yo", [128, 4, D], F32)
                    cvall = [sb3(f"cv{i}", [128, 512], F32) for i in range(8)]
                    cwo = PP_CW + (132 if revs[j] else 0)
                    MEMSET("pool", yt[:], 0.0, ["yt"])
                    s = 0
                    while s < nout:
                        no = min(510, nout - s)
                        ni = no + 2
                        lo_tok = max(s - 1, 0)
                        hi_tok = min(s + no + 1, S)
                        c_lo = lo_tok - (s - 1)
                        c_hi = hi_tok - (s - 1)
                        edge = (c_lo > 0) or (c_hi < ni)
                        stage_a(X1, lo_tok, c_lo, c_hi, 1, edge)
                        nb = (no + 127) // 128
                        for b in range(nb):
                            rows = min(128, no - b * 128)
                            P.dma("sp", "xr", xres[0:rows, b, :], X1[s + b * 128:s + b * 128 + rows, :], writes=["xres"])
                        for k in range(11):
                            wu, wuk = load_piece(b_up[:, k * 512:(k + 1) * 512].rearrange("(c p) n -> p c n", p=128),
                                                 wview_d(512))
                            pbase = 4 * (k % 2)
                            for cc in range(4):
                                for dc in range(DC):
                                    MM(bank(pbase + cc, ni), wu[:, dc, cc * 128:(cc + 1) * 128], hT[:, dc, 0:ni],
                                       dc == 0, dc == DC - 1, [wuk, "hT"], [BK[pbase + cc]])
                            for pr in range(2):
                                jf = 2 * k + pr
                                cv = cvall[4 * (jf % 2):4 * (jf % 2) + 4]
                                cvk = [f"cv{4 * (jf % 2) + i}" for i in range(4)]
                                for vi, cc in enumerate((pr, 2 + pr)):
                                    col = 4 * k + cc
                                    bk_ = pbase + cc
                                    u = ps[:, bk_ * 512:bk_ * 512 + 512]
                                    tdst = cv[vi]
                                    ACT(tdst[:, 0:no], u[:, 0:no], AF.Identity, [BK[bk_], "pp"], [cvk[vi]],
                                        scale=pp[:, cwo + col:cwo + col + 1],
                                        bias=pp[:, PP_CB + col:PP_CB + col + 1])
                                    STT("dve", tdst[:, 0:no], u[:, 1:no + 1], pp[:, cwo + 44 + col:cwo + 45 + col],
                                        tdst[:, 0:no], ALU.mult, ALU.add, [BK[bk_], "pp", cvk[vi]], [cvk[vi]])
                                    STT("dve", tdst[:, 0:no], u[:, 2:no + 2], pp[:, cwo + 88 + col:cwo + 89 + col],
                                        tdst[:, 0:no], ALU.mult, ALU.add, [BK[bk_], "pp", cvk[vi]], [cvk[vi]])
                                ACT(cv[2][:, 0:no], cv[1][:, 0:no], AF.Tanh, [cvk[1]], [cvk[2]], scale=0.5)
                                TS("pool", cv[2][:, 0:no], cv[2][:, 0:no], 0.5, 0.5, ALU.mult, ALU.add, [cvk[2]], [cvk[2]])
                                TT("pool", cv[3][:, 0:no], cv[2][:, 0:no], cv[1][:, 0:no], ALU.mult, [cvk[2], cvk[1]], [cvk[3]])
                                TT("pool", aT[:, jf, 0:no], cv[3][:, 0:no], cv[0][:, 0:no], ALU.mult,
                                   [cvk[3], cvk[0]], [("aT", jf)])
                        for nh in range(2):
                            for jh, (j0, jn) in enumerate(((0, 8), (8, 8), (16, 6))):
                                wd, wdk = load_piece(
                                    b_dn[j0 * 128:(j0 + jn) * 128, nh * 512:(nh + 1) * 512]
                                    .rearrange("(c p) n -> p c n", p=128),
                                    lambda tt, jn=jn: tt[:, 0:jn * 512].rearrange("p (c n) -> p c n", c=jn))
                                for b in range(nb):
                                    rows = min(128, no - b * 128)
                                    for jj in range(jn):
                                        jf = j0 + jj
                                        MM(bank(b, 512, rows), aT[:, jf, b * 128:b * 128 + rows], wd[:, jj, :],
                                           jf == 0, jf == NFF - 1, [wdk, ("aT", jf)], [BK[b]])
                            for b in range(nb):
                                rows = min(128, no - b * 128)
                                TT("dve", yt[0:rows, b, nh * 512:(nh + 1) * 512], bank(b, 512, rows),
                                   xres[0:rows, b, nh * 512:(nh + 1) * 512], ALU.add, [BK[b], "xres"], ["yt"])
                        norm_tm(yt, "yt", 2, yo, "yo")
                        for b in range(nb):
                            rows = min(128, no - b * 128)
                            P.dma("pool", "yst", y[s + b * 128:s + b * 128 + rows, :], yo[0:rows, b, :], reads=["yo"],
                                  writes=[("y", j)])
                        s += no
                    P.barrier()
        except _StopBuild:
            pass
        P.limit = stop if (stop is not None and stop > 100) else None
        P.emit()
    return nc


def _rope_tables(pos):
    S = pos.shape[0]
    posf = pos.astype(np.float32)
    out = np.zeros((4, 128, S), np.float32)
    inv_a = (1.0 / (np.float32(500000.0) ** (np.arange(0, 16, 2, dtype=np.float32) / np.float32(16)))).astype(np.float32)
    ang = posf[None, :] * inv_a[:, None]
    ca, sa = np.cos(ang).astype(np.float32), np.sin(ang).astype(np.float32)
    out[0] = 1.0
    for half in range(2):
        b = 64 * half
        out[0, b:b + 8] = ca
        out[0, b + 8:b + 16] = ca
        out[1, b:b + 8] = -sa
        out[1, b + 8:b + 16] = sa
    inv_b = (1.0 / (np.float32(10000.0) ** (np.arange(0, 32, 2, dtype=np.float32) / np.float32(32)))).astype(np.float32)
    rows = (pos // GRID_W).astype(np.float32)
    cols = (pos % GRID_W).astype(np.float32)
    for half in range(2):
        b = 64 * half
        for k, pp_ in enumerate((rows, cols)):
            ang = pp_[None, :] * inv_b[:, None]
            c, s_ = np.cos(ang).astype(np.float32), np.sin(ang).astype(np.float32)
            o = b + 32 * k
            out[2, o:o + 16] = c
            out[2, o + 16:o + 32] = c
            out[3, o:o + 16] = -s_
            out[3, o + 16:o + 32] = s_
    return out


def _const_mats():
    cm = np.zeros((128, NCM, 128), np.float32)
    cm[:, CM_ID, :] = np.eye(128, dtype=np.float32)
    for m in range(128):
        d = m % 64
        if d < 8:
            cm[m + 8, CM_RA, m] = 1.0
        elif d < 16:
            cm[m - 8, CM_RA, m] = 1.0
        dd = d % 32
        if dd < 16:
            cm[m + 16, CM_RB, m] = 1.0
        else:
            cm[m - 16, CM_RB, m] = 1.0
    cm[0:64, CM_S0, :] = 1.0
    cm[64:128, CM_S1, :] = 1.0
    cm[0:64, CM_BLK, 0:64] = 1.0
    cm[64:128, CM_BLK, 64:128] = 1.0
    cm[:, CM_ONE, :] = 1.0
    cm[64, CM_SELB, 0:64] = 1.0
    return cm


def _prep_shared(inp):
    f = lambda a: np.ascontiguousarray(np.asarray(a, dtype=np.float32))
    w_in = f(inp["w_in"])[0]
    qa, ka, va = w_in[:, 0:1024], w_in[:, 1024:2048], w_in[:, 2048:3072]
    qg, kg, vg = w_in[:, 3072:4096], w_in[:, 4096:4352], w_in[:, 4352:4608]
    gates = w_in[:, 4608:6656]
    order = []
    for pair in range(2):
        for i in range(4):
            order += [4 * (2 * pair) + i, 4 * (2 * pair + 1) + i]
    qg_p = np.concatenate([qg[:, h * 64:(h + 1) * 64] for h in order], axis=1)
    w_in_p = np.ascontiguousarray(np.concatenate([qa, qg_p, gates, ka, kg, va, vg], axis=1))
    w_up = f(inp["w_up"])[0]
    perm = []
    for k in range(11):
        for cc in (2 * k, 2 * k + 1):
            perm += list(range(cc * 128, cc * 128 + 128))
        for cc in (2 * k, 2 * k + 1):
            perm += list(range(DFF + cc * 128, DFF + cc * 128 + 128))
    perm = np.array(perm)
    w_up_p = np.ascontiguousarray(w_up[:, perm])
    conv_w = f(inp["conv_w"])[0][:, perm]
    conv_b = f(inp["conv_b"])[0][perm]
    pp = np.zeros((128, NPP), np.float32)
    pp[:, PP_SUBLN] = f(inp["subln_g"])[0]
    pp[:, PP_QN] = np.tile(f(inp["q_norm_g"])[0], 2)
    pp[:, PP_KN] = np.tile(f(inp["k_norm_g"])[0], 2)
    pp[:, PP_EPS] = EPS
    for i, nm in enumerate(("lam_q1", "lam_k1", "lam_q2", "lam_k2")):
        pp[:, PP_LAM + 64 * i:PP_LAM + 64 * (i + 1)] = f(inp[nm])[0][None, :]
    cw = conv_w.reshape(3, 44, 128).transpose(2, 0, 1)
    pp[:, PP_CW:PP_CW + 132] = cw.reshape(128, 132)
    pp[:, PP_CW + 132:PP_CW + 264] = cw[:, ::-1, :].reshape(128, 132)
    pp[:, PP_CB:PP_CB + 44] = conv_b.reshape(44, 128).T
    gvec = np.stack([np.broadcast_to(f(inp["norm_mix_g"])[0], (128, D)),
                     np.broadcast_to(f(inp["norm_ffn_g"])[0], (128, D)),
                     np.broadcast_to(f(inp["norm_final_g"]), (128, D))]).astype(np.float32)
    return dict(w_in=w_in_p, w_pa=f(inp["w_proj_a"])[0], w_pb=f(inp["w_proj_b"])[0], w_o=f(inp["w_out"])[0],
                w_up=w_up_p, w_dn=f(inp["w_down"])[0], gvec=np.ascontiguousarray(gvec), pp=pp, cm=_const_mats())


_CACHE = {}


def kernel(**inputs):
    S = inputs["x_prompt"].shape[1]
    xp = np.asarray(inputs["x_prompt"], dtype=np.float32)
    xsm = np.asarray(inputs["x_sample"], dtype=np.float32)
    seqs = [xp[i] for i in range(xp.shape[0])] + [xsm[i] for i in range(xsm.shape[0])]
    nseq = len(seqs)
    assert nseq == 12 and S % 1024 == 0
    H = S // 2
    jobs = [(S // 512, S), (H // 512 + 1, H)]
    key = (S, tuple(jobs))
    if key not in _CACHE:
        _CACHE[key] = build_program(S, jobs, revs=(False, True))
    nc = _CACHE[key]
    shared = _prep_shared(inputs)
    tab_f = _rope_tables(np.arange(S))
    tab_r = _rope_tables(np.arange(S)[::-1].copy())
    pp_even = shared["pp"].copy()
    pp_even[:, PP_CW + 132:PP_CW + 264] = pp_even[:, PP_CW:PP_CW + 132]
    pp_odd = shared["pp"]
    in_maps = []
    for c in range(N_CORES):
        p, odd = c // 2, c % 2
        m = dict(shared)
        m["x0"] = np.ascontiguousarray(seqs[3 * p + odd])
        sh = seqs[3 * p + 2]
        m["x1"] = np.ascontiguousarray(sh[::-1]) if odd else np.ascontiguousarray(sh)
        m["tab0"] = tab_f
        m["tab1"] = tab_r if odd else tab_f
        m["pp"] = pp_odd if odd else pp_even
        in_maps.append(m)
    res = run_bass_kernel_spmd(nc, in_maps, core_ids=list(range(N_CORES)))
    outs = [None] * nseq
    for p in range(N_CORES // 2):
        outs[3 * p] = res.results[2 * p]["y0"]
        outs[3 * p + 1] = res.results[2 * p + 1]["y0"]
        lo = res.results[2 * p]["y1"]
        hi = res.results[2 * p + 1]["y1"][::-1]
        outs[3 * p + 2] = np.concatenate([lo, hi], axis=0)
    nb = xp.shape[0]
    y_prompt = np.stack(outs[:nb]).astype(np.float32)
    y_sample = np.stack(outs[nb:]).astype(np.float32)
    return (y_prompt, y_sample)
```

```python
import contextlib
import numpy as np
import concourse.bass as bass
import concourse.mybir as mybir
from concourse.bass_utils import run_bass_kernel_spmd

F32 = mybir.dt.float32
BF16 = mybir.dt.bfloat16
AF = mybir.ActivationFunctionType
ALU = mybir.AluOpType
AX = mybir.AxisListType

D = 1024
DC = 8
DFF = 2816
NFF = 22
D_IN = 6656
EPS = 1e-6
GRID_W = 64
N_CORES = 8
ENGS = ("pe", "act", "dve", "pool", "sp")
SEM_EPOCH = 24000
DUMMY_MM = 0

OQ, OG, OK_, OV = 0, 2048, 4096, 5376


class _StopBuild(Exception):
    pass


class Op:
    __slots__ = ("eng", "fn", "deps", "key", "inc", "needs_inc", "epoch", "value", "idx")


class Prog:
    def __init__(self, nc):
        self.nc = nc
        self.q = {e: [] for e in ENGS}
        self.res_w = {}
        self.res_r = {}
        self.last_by_key = {}
        self.pending_barrier = {}
        self.n_ops = 0
        self.limit = None

    def _record(self, eng, fn, reads, writes, key=None, inc=1):
        op = Op()
        op.eng = eng
        op.fn = fn
        is_dma = key is not None
        op.key = key if is_dma else eng
        op.inc = inc
        op.needs_inc = is_dma
        op.epoch = 0
        op.value = 0
        op.idx = self.n_ops
        self.n_ops += 1
        deps = {}
        me = (eng, key) if is_dma else eng
        for r in reads:
            w = self.res_w.get(r)
            if w:
                for o in w.values():
                    deps[id(o)] = o
        for r in writes:
            rd = self.res_r.get(r)
            if rd:
                for e, o in rd.items():
                    deps[id(o)] = o
            w = self.res_w.get(r)
            if w:
                for e, o in w.items():
                    if e != me or is_dma or eng != "pe":
                        deps[id(o)] = o
        pb = self.pending_barrier.pop(eng, None)
        if pb:
            for o in pb:
                deps[id(o)] = o
        op.deps = list(deps.values())
        for o in op.deps:
            o.needs_inc = True
        for r in reads:
            self.res_r.setdefault(r, {})[me] = op
        for r in writes:
            if self.res_r.get(r):
                self.res_w[r] = {me: op}
                self.res_r[r] = {}
            else:
                self.res_w.setdefault(r, {})[me] = op
        self.q[eng].append(op)
        self.last_by_key[op.key] = op
        return op

    def op(self, eng, fn, reads=(), writes=()):
        return self._record(eng, fn, reads, writes)

    def dma(self, eng, key, out, in_, reads=(), writes=()):
        return self._record(eng, lambda e: e.dma_start(out=out, in_=in_), reads, writes,
                            key=("dma", key), inc=16)

    def barrier(self):
        lasts = list(self.last_by_key.values())
        for e in ENGS:
            self.pending_barrier[e] = list(lasts)

    def emit(self):
        nc = self.nc
        if self.limit is not None:
            for e in ENGS:
                self.q[e] = [o for o in self.q[e] if o.idx < self.limit]
        print("n_ops", self.n_ops, {e: len(self.q[e]) for e in ENGS})
        keycount = {}
        sems = {}
        for e in ENGS:
            for op in self.q[e]:
                if not op.needs_inc:
                    continue
                ep, cnt = keycount.get(op.key, (0, 0))
                if cnt + op.inc > SEM_EPOCH:
                    ep, cnt = ep + 1, 0
                cnt += op.inc
                keycount[op.key] = (ep, cnt)
                op.epoch, op.value = ep, cnt
                sems[(op.key, ep)] = None
        last = {}
        for e in ENGS:
            for op in self.q[e]:
                if op.needs_inc:
                    k = (op.key, op.epoch)
                    last[k] = max(last.get(k, 0), op.value)
        with contextlib.ExitStack() as st:
            for i, k in enumerate(list(sems.keys())):
                sems[k] = st.enter_context(nc.semaphore(f"s{i}"))
            block = st.enter_context(nc.Block())

            def run(name, eng):
                waited = {}
                for op in self.q[name]:
                    need = {}
                    for d in op.deps:
                        k = (d.key, d.epoch)
                        if waited.get(k, 0) >= d.value:
                            continue
                        if need.get(k, 0) < d.value:
                            need[k] = d.value
                    for k, v in need.items():
                        eng.wait_ge(sems[k], v)
                        waited[k] = v
                    ins = op.fn(eng)
                    if op.needs_inc:
                        ins.then_inc(sems[(op.key, op.epoch)], op.inc)
                if name == "sp":
                    for k, v in last.items():
                        if waited.get(k, 0) < v:
                            eng.wait_ge(sems[k], v)

            @block.tensor
            def _(eng):
                run("pe", eng)

            @block.scalar
            def _(eng):
                run("act", eng)

            @block.vector
            def _(eng):
                run("dve", eng)

            @block.gpsimd
            def _(eng):
                run("pool", eng)

            @block.sync
            def _(eng):
                run("sp", eng)


PP_SUBLN, PP_QN, PP_KN, PP_EPS, PP_LAM = 0, 1, 2, 3, 4
PP_CW = 4 + 256
PP_CB = PP_CW + 2 * 3 * 44
NPP = PP_CB + 44
CM_ID, CM_RA, CM_RB, CM_S0, CM_S1, CM_BLK, CM_ONE, CM_SELB = range(8)
NCM = 8


def build_program(S, jobs, revs=(False, False), stop=None):
    NT = S // 512
    KC = S // 128
    SEG = min(1024, S)
    NSEG = S // SEG
    SEGC = SEG // 128
    nc = bass.Bass("TRN2", target_bir_lowering=False)
    nj = len(jobs)

    def dram(name, shape, dt, kind):
        return nc.dram_tensor(name, shape, dt, kind=kind).ap()

    xs = [dram(f"x{j}", [S, D], F32, "ExternalInput") for j in range(nj)]
    tabs = [dram(f"tab{j}", [4, 128, S], F32, "ExternalInput") for j in range(nj)]
    ys = [dram(f"y{j}", [jobs[j][1], D], F32, "ExternalOutput") for j in range(nj)]
    w_in = dram("w_in", [D, D_IN], F32, "ExternalInput")
    w_pa = dram("w_pa", [D, D], F32, "ExternalInput")
    w_pb = dram("w_pb", [D, D], F32, "ExternalInput")
    w_o = dram("w_o", [D, D], F32, "ExternalInput")
    w_up = dram("w_up", [D, 2 * DFF], F32, "ExternalInput")
    w_dn = dram("w_dn", [DFF, D], F32, "ExternalInput")
    gvec = dram("gvec", [3, 128, D], F32, "ExternalInput")
    ppd = dram("pp", [128, NPP], F32, "ExternalInput")
    cmd = dram("cm", [128, NCM, 128], F32, "ExternalInput")
    b_in = dram("b_in", [D, D_IN], BF16, "Internal")
    b_pa = dram("b_pa", [D, D], BF16, "Internal")
    b_pb = dram("b_pb", [D, D], BF16, "Internal")
    b_o = dram("b_o", [D, D], BF16, "Internal")
    b_up = dram("b_up", [D, 2 * DFF], BF16, "Internal")
    b_dn = dram("b_dn", [DFF, D], BF16, "Internal")
    KT = dram("KT", [10, 128, S], BF16, "Internal")
    VA = dram("VA", [8, S, 128], BF16, "Internal")
    VG = dram("VG", [4, S, 64], BF16, "Internal")
    X1 = dram("X1", [S, D], F32, "Internal")

    P = Prog(nc)
    st = contextlib.ExitStack()
    with st:
        def sb(name, shape, dt):
            return st.enter_context(nc.sbuf_tensor("s_" + name, shape, dt))

        pp = sb("pp", [128, NPP], F32)
        cmf = sb("cmf", [128, NCM, 128], F32)
        cmb = sb("cmb", [128, NCM, 128], BF16)
        gv = sb("gv", [128, 3, D], F32)
        lamt = sb("lamt", [128, 8], F32)
        mk = sb("mk", [128, 32], F32)
        mke = sb("mke", [128, 32], F32)
        mq = sb("mq", [128, 32], F32)
        negc = sb("negc", [128, 32], F32)
        negcu = sb("negcu", [128, 16], F32)
        small = sb("small", [128, 16], F32)
        mk_tmp = sb("mk_tmp", [128, 8], F32)
        xt = sb("xt", [128, 4, D], F32)
        hb = sb("hb", [128, 4, D], BF16)
        hT = sb("hT", [128, DC, 512], BF16)
        wbuf = [sb(f"wbuf{i}", [128, 4096], BF16) for i in range(3)]
        tb = sb("tb", [128, 4, 512], F32)
        NSET = 3
        tmpf = [sb(f"tmpf{i}", [128, 512], F32) for i in range(4 * NSET)]
        tmpb = [sb(f"tmpb{i}", [128, 512], BF16) for i in range(3 * NSET)]
        rot = {"i": 0}
        SCR = [(0, 1), (2, 3), (6, 7)]
        ps = st.enter_context(nc.psum_tensor("ps", [128, 4096], F32))

        def bank(i, n=512, parts=128):
            return ps[0:parts, i * 512:i * 512 + n]

        BK = [f"B{i}" for i in range(8)]
        ident = cmb[:, CM_ID, :]

        def MM(out, lhsT, rhs, start, stop, r, w):
            P.op("pe", lambda e: e.matmul(out, lhsT=lhsT, rhs=rhs, start=start, stop=stop), r, w)

        def TR(out, in_, r, w):
            P.op("pe", lambda e: e.transpose(out=out, in_=in_, identity=ident), r, w)

        def ACT(out, in_, func, r, w, scale=1.0, bias=None, accum=None):
            kw = {}
            if bias is not None:
                kw["bias"] = bias
            if accum is not None:
                kw["accum_out"] = accum
            P.op("act", lambda e: e.activation(out=out, in_=in_, func=func, scale=scale, **kw), r, w)

        def TT(eng, out, in0, in1, op, r, w):
            P.op(eng, lambda e: e.tensor_tensor(out=out, in0=in0, in1=in1, op=op), r, w)

        def STT(eng, out, in0, scalar, in1, op0, op1, r, w):
            P.op(eng, lambda e: e.scalar_tensor_tensor(out=out, in0=in0, scalar=scalar, in1=in1,
                                                       op0=op0, op1=op1), r, w)

        def TS(eng, out, in0, s1, s2, op0, op1, r, w):
            P.op(eng, lambda e: e.tensor_scalar(out=out, in0=in0, scalar1=s1, scalar2=s2,
                                                op0=op0, op1=op1), r, w)

        def TS1(eng, out, in0, s1, op, r, w):
            P.op(eng, lambda e: e.tensor_single_scalar(out=out, in_=in0, scalar=s1, op=op), r, w)

        def CP(eng, out, in_, r, w):
            if eng == "act":
                P.op("act", lambda e: e.copy(out=out, in_=in_), r, w)
            else:
                P.op(eng, lambda e: e.tensor_copy(out=out, in_=in_), r, w)

        def MEMSET(eng, ap, val, w):
            P.op(eng, lambda e: e.memset(ap, val), (), w)

        def RMAX(out, in_, r, w):
            P.op("dve", lambda e: e.reduce_max(out=out, in_=in_, axis=AX.X), r, w)

        def RSUM(out, in_, r, w):
            P.op("dve", lambda e: e.reduce_sum(out=out, in_=in_, axis=AX.X), r, w)

        def RECIP(out, in_, r, w):
            P.op("dve", lambda e: e.reciprocal(out=out, in_=in_), r, w)

        dmaq = ["sp", "pool"]

        P.dma("sp", "c0", pp[:], ppd[:, :], writes=["pp"])
        P.dma("sp", "c1", cmf[:], cmd[:, :, :], writes=["cmf"])
        P.dma("sp", "c2", gv[:], gvec.rearrange("g p d -> p g d"), writes=["gv"])
        CP("dve", cmb[:], cmf[:], ["cmf"], ["cmb"])
        for i in range(2):
            TT("dve", tmpf[0][:, 0:64], pp[:, PP_LAM + 128 * i:PP_LAM + 128 * i + 64],
               pp[:, PP_LAM + 128 * i + 64:PP_LAM + 128 * i + 128], ALU.mult, ["pp"], ["tmpf0"])
            RSUM(lamt[:, i:i + 1], tmpf[0][:, 0:64], ["tmpf0"], ["lamt"])
        ACT(lamt[:, 2:4], lamt[:, 0:2], AF.Exp, ["lamt"], ["lamt"])
        TT("dve", lamt[:, 4:5], lamt[:, 3:4], lamt[:, 2:3], ALU.subtract, ["lamt"], ["lamt"])
        TS1("dve", lamt[:, 5:6], lamt[:, 4:5], -0.2, ALU.add, ["lamt"], ["lamt"])
        TS1("dve", lamt[:, 6:7], pp[:, PP_SUBLN:PP_SUBLN + 1], 0.8, ALU.mult, ["pp", "lamt"], ["lamt"])
        neglam = lamt[:, 5:6]
        gsub = lamt[:, 6:7]
        eps_ap = pp[:, PP_EPS:PP_EPS + 1]

        wi = 0
        for (src, dst, rows, cols) in ((w_in, b_in, D, D_IN), (w_pa, b_pa, D, D), (w_pb, b_pb, D, D),
                                       (w_o, b_o, D, D), (w_up, b_up, D, 2 * DFF), (w_dn, b_dn, DFF, D)):
            for r0 in range(0, rows, 128):
                for c0 in range(0, cols, 4096):
                    cw = min(4096, cols - c0)
                    slot = wi % 3
                    P.dma("pool", f"wc{slot}", wbuf[slot][:, 0:cw], src[r0:r0 + 128, c0:c0 + cw],
                          writes=[f"wbuf{slot}"])
                    P.dma("sp", f"ws{slot}", dst[r0:r0 + 128, c0:c0 + cw], wbuf[slot][:, 0:cw],
                          reads=[f"wbuf{slot}"], writes=["wscratch"])
                    wi += 1
        P.barrier()
        if stop == 0:
            jobs = []

        wstate = {"i": 0}

        def load_piece(src_ap, view):
            slot = wstate["i"] % 3
            wstate["i"] += 1
            dst = view(wbuf[slot])
            P.dma("sp", f"wl{slot}", dst, src_ap, reads=["wscratch"], writes=[f"wbuf{slot}"])
            return dst, f"wbuf{slot}"

        def wview_d(cols):
            return lambda t: t[:, 0:DC * cols].rearrange("p (c n) -> p c n", c=DC)

        def piece_in(c0, cols):
            return load_piece(b_in[:, c0:c0 + cols].rearrange("(c p) n -> p c n", p=128), wview_d(cols))

        def stage_a(src, row0, c_lo, c_hi, gi, zero_first):
            if zero_first:
                MEMSET("pool", xt[:], 0.0, ["xt"])
            for b in range(4):
                lo, hi = max(c_lo, 128 * b), min(c_hi, 128 * b + 128)
                if lo >= hi:
                    continue
                P.dma("sp", "xl", xt[lo - 128 * b:hi - 128 * b, b, :],
                      src[row0 + lo - c_lo:row0 + hi - c_lo, :], writes=["xt"])
            norm_tm(xt, "xt", gi, hb, "hb")
            nb = (c_hi + 127) // 128
            psT = ps[:, 0:2048].bitcast(BF16).rearrange("p (c n) -> p c n", c=DC)
            for b in range(nb):
                for c in range(DC):
                    TR(psT[:, c, b * 128:(b + 1) * 128], hb[:, b, c * 128:(c + 1) * 128],
                       ["hb", "cmb"], [BK[c // 2]])
            w = nb * 128
            for c2 in range(4):
                eng = "dve" if c2 % 2 == 0 else "act"
                CP(eng, hT[:, 2 * c2:2 * c2 + 2, 0:w], psT[:, 2 * c2:2 * c2 + 2, 0:w], [BK[c2]], ["hT"])

        def norm_tm(src_t, src_key, gi, out_t, out_key):
            for b in range(4):
                ACT(hb[:, b, :], src_t[:, b, :], AF.Square, [src_key], ["hb", "ss"], accum=small[:, b:b + 1])
            ACT(small[:, 4:8], small[:, 0:4], AF.Ln, ["ss"], ["ss2"], scale=1.0 / D, bias=eps_ap)
            ACT(small[:, 8:12], small[:, 4:8], AF.Exp, ["ss2"], ["rstd"], scale=-0.5)
            for b in range(4):
                STT("dve", out_t[:, b, :], src_t[:, b, :], small[:, 8 + b:9 + b],
                    gv[:, gi, :], ALU.mult, ALU.mult, [src_key, "rstd", "gv", "hb"], [out_key])

        def finish_qk(psq_bank, is_b, gcol, scale, out_ap, out_key, bound_dst, bound_col, running):
            st_ = rot["i"] % NSET
            rot["i"] += 1
            tf = [tmpf[4 * st_ + i] for i in range(4)]
            tfk = [f"tmpf{4 * st_ + i}" for i in range(4)]
            tbb = [tmpb[3 * st_ + i] for i in range(3)]
            tbk = [f"tmpb{3 * st_ + i}" for i in range(3)]
            b0, b1 = SCR[st_]
            if is_b:
                ACT(tbb[0][:], bank(psq_bank), AF.Square, [BK[psq_bank]], [tbk[0]])
                CP("dve", tf[0][:], bank(psq_bank), [BK[psq_bank], tbk[0]], [tfk[0]])
                MM(bank(b0), cmb[:, CM_BLK, :], tbb[0][:], True, True, [tbk[0], "cmb"], [BK[b0]])
                ACT(tf[1][:], bank(b0), AF.Ln, [BK[b0]], [tfk[1]], scale=1.0 / 64, bias=eps_ap)
                ACT(tf[1][:], tf[1][:], AF.Exp, [tfk[1]], [tfk[1]], scale=-0.5)
                STT("dve", tbb[1][:], tf[0][:], pp[:, gcol:gcol + 1], tf[1][:], ALU.mult, ALU.mult,
                    [tfk[0], tfk[1], "pp"], [tbk[1]])
                rm, ci, si = CM_RB, 2, 3
            else:
                CP("act", tbb[1][:], bank(psq_bank), [BK[psq_bank]], [tbk[1]])
                rm, ci, si = CM_RA, 0, 1
            MM(bank(b1), cmb[:, rm, :], tbb[1][:], True, True, [tbk[1], "cmb"], [BK[b1]])
            if scale == 1.0:
                TT("pool", tf[2][:], tbb[1][:], tb[:, ci, :], ALU.mult, [tbk[1], "tb"], [tfk[2]])
                TT("dve", tf[3][:], bank(b1), tb[:, si, :], ALU.mult, [BK[b1], "tb"], [tfk[3]])
            else:
                STT("dve", tf[2][:], tbb[1][:], scale, tb[:, ci, :], ALU.mult, ALU.mult, [tbk[1], "tb"], [tfk[2]])
                STT("dve", tf[3][:], bank(b1), scale, tb[:, si, :], ALU.mult, ALU.mult, [BK[b1], "tb"], [tfk[3]])
            TT("pool", out_ap, tf[2][:], tf[3][:], ALU.add, [tfk[2], tfk[3]], [out_key])
            TT("pool", tbb[2][:], out_ap, out_ap, ALU.mult, [out_key], [tbk[2]])
            for h in range(2):
                bb = (b0, b1)[h]
                MM(bank(bb), cmb[:, CM_S0 + h, :], tbb[2][:], True, True, [tbk[2], "cmb"], [BK[bb]])
                if running:
                    sc = small[:, 12 + 2 * st_ + h - 0:13 + 2 * st_ + h - 0] if False else mk_tmp[:, 2 * st_ + h:2 * st_ + h + 1]
                    RMAX(sc, bank(bb), [BK[bb]], [("mkt", st_, h)])
                    TT("dve", bound_dst[:, bound_col + h:bound_col + h + 1], bound_dst[:, bound_col + h:bound_col + h + 1],
                       sc, ALU.max, [("mkt", st_, h), ("mkq", bound_col + h)], [("mkq", bound_col + h)])
                else:
                    RMAX(bound_dst[:, bound_col + h:bound_col + h + 1], bank(bb), [BK[bb]], [("mkq", bound_col + h)])

        try:
            for j, (nq, nout) in enumerate(jobs):
                x, tab, y = xs[j], tabs[j], ys[j]
                with contextlib.ExitStack() as p1:
                    wk = p1.enter_context(nc.sbuf_tensor(f"wk{j}", [128, DC, 1280], BF16))
                    wv = p1.enter_context(nc.sbuf_tensor(f"wv{j}", [128, DC, 1280], BF16))
                    kout = [p1.enter_context(nc.sbuf_tensor(f"kout{j}_{i}", [128, 512], BF16)) for i in range(3)]
                    vsb = [p1.enter_context(nc.sbuf_tensor(f"vsb{j}_{i}", [128, 1280], BF16)) for i in range(2)]
                    P.dma("sp", "wk", wk[:], b_in[:, OK_:OK_ + 1280].rearrange("(c p) n -> p c n", p=128),
                          reads=["wscratch"], writes=["wk"])
                    P.dma("sp", "wv", wv[:], b_in[:, OV:OV + 1280].rearrange("(c p) n -> p c n", p=128),
                          reads=["wscratch"], writes=["wv"])
                    MEMSET("dve", mk[:], 0.0, [("mkq", c_) for c_ in range(32)])
                    for t in range(NT):
                        s = t * 512
                        stage_a(x, s, 0, 512, 0, False)
                        if stop == 10:
                            raise _StopBuild()
                        P.dma("sp", "tb", tb[:], tab[:, :, s:s + 512].rearrange("f p n -> p f n"), writes=["tb"])
                        for kc in range(10):
                            if stop == 11 and kc == 1:
                                raise _StopBuild()
                            if stop == 12 and kc == 9:
                                raise _StopBuild()
                            pb_ = 4 + (kc % 2)
                            for dc in range(DC):
                                MM(bank(pb_), wk[:, dc, kc * 128:(kc + 1) * 128], hT[:, dc, :], dc == 0, dc == DC - 1,
                                   ["wk", "hT"], [BK[pb_]])
                            ko = kout[kc % 3]
                            kkey = f"kout{kc % 3}"
                            finish_qk(pb_, kc >= 8, PP_KN, 1.0, ko[:], kkey, mk, 2 * kc, True)
                            P.dma("pool", f"kst{kc % 3}", KT[kc, :, s:s + 512], ko[:], reads=[kkey],
                                  writes=[("KT", kc, s // SEG)])
                        if stop == 13:
                            raise _StopBuild()
                        for b in range(4):
                            vs = vsb[b % 2]
                            vkey = f"vsb{b % 2}"
                            for pi, (c0, cw) in enumerate(((0, 512), (512, 512), (1024, 256))):
                                for dc in range(DC):
                                    MM(bank(pi, cw), hT[:, dc, b * 128:(b + 1) * 128], wv[:, dc, c0:c0 + cw],
                                       dc == 0, dc == DC - 1, ["wv", "hT"], [BK[pi]])
                                CP("act" if pi == 1 else "dve", vs[:, c0:c0 + cw], bank(pi, cw), [BK[pi]], [vkey])
                            r0 = s + b * 128
                            P.dma("pool", f"vst{b % 2}", VA[:, r0:r0 + 128, :].rearrange("h p d -> p h d"),
                                  vs[:, 0:1024].rearrange("p (h d) -> p h d", h=8), reads=[vkey],
                                  writes=[("VA", r0 // SEG)])
                            P.dma("pool", f"vsg{b % 2}", VG[:, r0:r0 + 128, :].rearrange("h p d -> p h d"),
                                  vs[:, 1024:1280].rearrange("p (h d) -> p h d", h=4), reads=[vkey],
                                  writes=[("VG", r0 // SEG)])
                    CP("dve", mke[:, 0:16], mk[:, 0:16], [("mkq", c_) for c_ in range(32)], ["mke"])
                    for jj in range(8):
                        CP("dve", mke[:, 16 + 2 * jj:18 + 2 * jj], mk[:, 16 + 2 * (jj // 4):18 + 2 * (jj // 4)],
                           [("mkq", c_) for c_ in range(32)], ["mke"])
                    P.barrier()
                if stop == 1:
                    raise _StopBuild()

                with contextlib.ExitStack() as p2:
                    def sb2(name, shape, dt):
                        return p2.enter_context(nc.sbuf_tensor(f"{name}{j}", shape, dt))
                    qT = sb2("qT", [128, 16, 512], BF16)
                    kseg = [sb2(f"kseg{i}", [128, SEG], BF16) for i in range(3)]
                    vseg = [sb2(f"vseg{i}", [128, SEGC, 128], BF16) for i in range(3)]
                    vsgg = [sb2(f"vsgg{i}", [128, 2, SEGC, 64], BF16) for i in range(3)]
                    pT = [sb2(f"pT{i}", [128, 1024], BF16) for i in range(3)]
                    osb = [sb2(f"osb{i}", [128, 512], F32) for i in range(2)]
                    oaT = sb2("oaT", [128, 8, 512], BF16)
                    obT = sb2("obT", [64, 16, 512], BF16)
                    mixT = hb[:].rearrange("p b d -> p (b d)").rearrange("p (c n) -> p c n", c=8)
                    x1t = sb2("x1t", [128, 2, D], F32)

                    for t in range(nq):
                        s = t * 512
                        stage_a(x, s, 0, 512, 0, False)
                        P.dma("sp", "tb", tb[:], tab[:, :, s:s + 512].rearrange("f p n -> p f n"), writes=["tb"])
                        for pc in range(4):
                            wp, wkey = piece_in(OQ + pc * 512, 512)
                            for cc in range(4):
                                c = pc * 4 + cc
                                pb_ = 4 + (c % 2)
                                for dc in range(DC):
                                    MM(bank(pb_), wp[:, dc, cc * 128:(cc + 1) * 128], hT[:, dc, :], dc == 0, dc == DC - 1,
                                       [wkey, "hT"], [BK[pb_]])
                                finish_qk(pb_, c >= 8, PP_QN, 0.125, qT[:, c, :], ("qT", c), mq, 2 * c, False)
                        TT("dve", negc[:], mq[:], mke[:], ALU.mult, [("mkq", c_) for c_ in range(32)] + ["mke"], ["negc"])
                        ACT(negc[:], negc[:], AF.Ln, ["negc"], ["negc"])
                        ACT(negc[:], negc[:], AF.Exp, ["negc"], ["negc"], scale=0.5)
                        TS1("dve", negc[:], negc[:], -1.0, ALU.mult, ["negc"], ["negc"])

                        units = []
                        for h in range(8):
                            units.append(dict(kind="A", kc=h, qc=h, out=h))
                        for p_ in range(2):
                            for i_ in range(4):
                                units.append(dict(kind="B", kc=8 + p_, qc=8 + 4 * p_ + i_, v=[2 * p_, 2 * p_ + 1],
                                                  out=[4 * (2 * p_) + i_, 4 * (2 * p_ + 1) + i_]))
                        TT("dve", negcu[:], negc[:, 0:32:2], negc[:, 1:32:2], ALU.min, ["negc"], ["negcu"])
                        segs = [(u, sg) for u in range(len(units)) for sg in range(NSEG)]

                        def load_seg(i):
                            u, sg = segs[i]
                            un = units[u]
                            sl = i % 3
                            P.dma("sp", f"ks{sl}", kseg[sl][:], KT[un["kc"], :, sg * SEG:(sg + 1) * SEG],
                                  reads=[("KT", un["kc"], sg)], writes=[f"kseg{sl}"])
                            if un["kind"] == "A":
                                P.dma("sp", f"vs{sl}", vseg[sl][:],
                                      VA[un["kc"], sg * SEG:(sg + 1) * SEG, :].rearrange("(c p) d -> p c d", p=128),
                                      reads=[("VA", sg)], writes=[f"vseg{sl}"])
                            else:
                                for jv in range(2):
                                    P.dma("sp", f"vs{sl}", vsgg[sl][:, jv, :, :],
                                          VG[un["v"][jv], sg * SEG:(sg + 1) * SEG, :].rearrange("(c p) d -> p c d", p=128),
                                          reads=[("VG", sg)], writes=[f"vsgg{sl}"])

                        steps = []
                        for i, (u, sg) in enumerate(segs):
                            for c_ in range(SEGC):
                                steps.append((i, u, sg, c_))
                        nsteps = len(steps)

                        def rec_qk(k):
                            i, u, sg, c_ = steps[k]
                            un = units[u]
                            sl = i % 3
                            stb = k % 2
                            for sj in range(2):
                                r0 = 64 * sj
                                MM(bank(2 * stb + sj), kseg[sl][r0:r0 + 64, c_ * 128:(c_ + 1) * 128],
                                   qT[r0:r0 + 64, un["qc"], :], True, True,
                                   [f"kseg{sl}", ("qT", un["qc"])], [BK[2 * stb + sj]])

                        def rec_rest(k):
                            i, u, sg, c_ = steps[k]
                            un = units[u]
                            sl = i % 3
                            stb = k % 2
                            pt = pT[k % 3]
                            ptk = f"pT{k % 3}"
                            qc = un["qc"]
                            ACT(pt[:], ps[:, stb * 1024:(stb + 1) * 1024], AF.Exp, [BK[2 * stb], BK[2 * stb + 1], "negcu"], [ptk],
                                bias=negcu[:, qc:qc + 1])
                            cg = sg * SEGC + c_
                            first = cg == 0
                            last = cg == NSEG * SEGC - 1
                            isA = un["kind"] == "A"
                            for sj in range(2):
                                if isA:
                                    lhsT = vseg[sl][:, c_, :]
                                    out = bank(4 + sj)
                                    vk = f"vseg{sl}"
                                else:
                                    lhsT = vsgg[sl][:, sj, c_, :]
                                    out = bank(4 + sj, 512, 64)
                                    vk = f"vsgg{sl}"
                                MM(out, lhsT, pt[:, sj * 512:(sj + 1) * 512], first, last, [vk, ptk], [BK[4 + sj]])
                            for sj in range(2):
                                MM(bank(6 + sj), cmb[:, CM_ONE, :], pt[:, sj * 512:(sj + 1) * 512],
                                   first, last, ["cmb", ptk], [BK[6 + sj]])
                            if last:
                                finalize_pair(u)

                        def finalize_pair(u):
                            un = units[u]
                            isA = un["kind"] == "A"
                            np_ = 128 if isA else 64
                            sets = []
                            for sj in range(2):
                                st_ = rot["i"] % NSET
                                rot["i"] += 1
                                tf = [tmpf[4 * st_ + i] for i in range(4)]
                                tfk = [f"tmpf{4 * st_ + i}" for i in range(4)]
                                sets.append((st_, tf, tfk))
                                CP("dve", tf[3][0:np_, :], bank(4 + sj, 512, np_), [BK[4 + sj]], [tfk[3]])
                                ACT(tf[0][0:np_, :], bank(6 + sj, 512, np_), AF.Ln, [BK[6 + sj]], [tfk[0]])
                            for sj in range(2):
                                st_, tf, tfk = sets[sj]
                                ACT(tf[0][0:np_, :], tf[0][0:np_, :], AF.Exp, [tfk[0]], [tfk[0]], scale=-1.0)
                                if isA:
                                    TT("dve", osb[sj][:], tf[3][:], tf[0][:], ALU.mult, [tfk[3], tfk[0]], [f"osb{sj}"])
                                else:
                                    hb_ = un["out"][sj]
                                    TT("dve", obT[:, hb_, :], tf[3][0:64, :], tf[0][0:64, :], ALU.mult,
                                       [tfk[3], tfk[0]], [("obT", hb_)])
                            if isA:
                                STT("dve", oaT[:, un["out"], :], osb[1][:], neglam, osb[0][:], ALU.mult, ALU.add,
                                    ["osb0", "osb1", "lamt"], [("oaT", un["out"])])

                        load_seg(0)
                        if len(segs) > 1:
                            load_seg(1)
                        rec_qk(0)
                        for k in range(nsteps):
                            i, u, sg, c_ = steps[k]
                            if c_ == 0 and i + 2 < len(segs):
                                load_seg(i + 2)
                            if k + 1 < nsteps:
                                rec_qk(k + 1)
                            rec_rest(k)

                        for h in range(8):
                            st_ = rot["i"] % NSET
                            rot["i"] += 1
                            tf = [tmpf[4 * st_ + i] for i in range(4)]
                            tfk = [f"tmpf{4 * st_ + i}" for i in range(4)]
                            tbs, tbsk = tmpb[3 * st_], f"tmpb{3 * st_}"
                            bq = SCR[st_][0]
                            TT("pool", tbs[:], oaT[:, h, :], oaT[:, h, :], ALU.mult, [("oaT", h)], [tbsk])
                            MM(bank(bq), cmb[:, CM_ONE, :], tbs[:], True, True, [tbsk, "cmb"], [BK[bq]])
                            ACT(tf[2][:], bank(bq), AF.Ln, [BK[bq]], [tfk[2]], scale=1.0 / 128, bias=eps_ap)
                            ACT(tf[2][:], tf[2][:], AF.Exp, [tfk[2]], [tfk[2]], scale=-0.5)
                            STT("dve", oaT[:, h, :], oaT[:, h, :], gsub, tf[2][:], ALU.mult, ALU.mult,
                                [("oaT", h), tfk[2], "lamt"], [("oaT", h)])
                        for m in range(8):
                            slot = wstate["i"] % 3
                            wstate["i"] += 1
                            tA = wbuf[slot]
                            wkA = f"wbuf{slot}"
                            wa = tA[:, 0:1024].rearrange("p (c n) -> p c n", c=DC)
                            wga = tA[:, 1024:2048].rearrange("p (c n) -> p c n", c=DC)
                            wgb = tA[:, 2048:3072].rearrange("p (c n) -> p c n", c=DC)
                            P.dma("sp", f"wl{slot}", wa, b_pa[:, m * 128:(m + 1) * 128].rearrange("(c p) n -> p c n", p=128),
                                  reads=["wscratch"], writes=[wkA])
                            P.dma("sp", f"wl{slot}", wga,
                                  b_in[:, OG + m * 128:OG + (m + 1) * 128].rearrange("(c p) n -> p c n", p=128),
                                  reads=["wscratch"], writes=[wkA])
                            P.dma("sp", f"wl{slot}", wgb,
                                  b_in[:, OG + 1024 + m * 128:OG + 1024 + (m + 1) * 128].rearrange("(c p) n -> p c n", p=128),
                                  reads=["wscratch"], writes=[wkA])
                            slot = wstate["i"] % 3
                            wstate["i"] += 1
                            wkB = f"wbuf{slot}"
                            wb_ = wbuf[slot][0:64, 0:2048].rearrange("p (h n) -> p h n", h=16)
                            P.dma("sp", f"wl{slot}", wb_, b_pb[:, m * 128:(m + 1) * 128].rearrange("(h p) n -> p h n", p=64),
                                  reads=["wscratch"], writes=[wkB])
                            st_ = m % NSET
                            tf = [tmpf[4 * st_ + i] for i in range(4)]
                            tfk = [f"tmpf{4 * st_ + i}" for i in range(4)]
                            q0 = 4 * (m % 2)
                            for dc in range(DC):
                                MM(bank(q0), wa[:, dc, :], oaT[:, dc, :], dc == 0, dc == DC - 1,
                                   [wkA, ("oaT", dc)], [BK[q0]])
                            for hh in range(16):
                                MM(bank(q0 + 1), wb_[:, hh, :], obT[:, hh, :], hh == 0, hh == 15,
                                   [wkB, ("obT", hh)], [BK[q0 + 1]])
                            for dc in range(DC):
                                MM(bank(q0 + 2), wga[:, dc, :], hT[:, dc, :], dc == 0, dc == DC - 1, [wkA, "hT"], [BK[q0 + 2]])
                            for dc in range(DC):
                                MM(bank(q0 + 3), wgb[:, dc, :], hT[:, dc, :], dc == 0, dc == DC - 1, [wkA, "hT"], [BK[q0 + 3]])
                            ACT(tf[0][:], bank(q0 + 2), AF.Tanh, [BK[q0 + 2]], [tfk[0]], scale=0.5)
                            ACT(tf[1][:], bank(q0 + 3), AF.Tanh, [BK[q0 + 3]], [tfk[1]], scale=0.5)
                            TS("pool", tf[0][:], tf[0][:], 0.5, 0.5, ALU.mult, ALU.add, [tfk[0]], [tfk[0]])
                            TS("pool", tf[1][:], tf[1][:], 0.5, 0.5, ALU.mult, ALU.add, [tfk[1]], [tfk[1]])
                            TT("dve", tf[2][:], tf[0][:], bank(q0), ALU.mult, [tfk[0], BK[q0]], [tfk[2]])
                            TT("dve", tf[3][:], tf[1][:], bank(q0 + 1), ALU.mult, [tfk[1], BK[q0 + 1]], [tfk[3]])
                            TT("pool", mixT[:, m, :], tf[2][:], tf[3][:], ALU.add, [tfk[2], tfk[3]], ["hb"])
                        wos = []
                        for nh in range(2):
                            wos.append(load_piece(b_o[:, nh * 512:(nh + 1) * 512].rearrange("(c p) n -> p c n", p=128),
                                                  wview_d(512)))
                        for b in range(4):
                            xb_ = b % 2
                            for nh in range(2):
                                wo, wok = wos[nh]
                                pb_ = 4 + 2 * xb_ + nh
                                for dc in range(DC):
                                    MM(bank(pb_), mixT[:, dc, b * 128:(b + 1) * 128], wo[:, dc, :], dc == 0, dc == DC - 1,
                                       [wok, "hb"], [BK[pb_]])
                                TT("dve", x1t[:, xb_, nh * 512:(nh + 1) * 512], bank(pb_), xt[:, b, nh * 512:(nh + 1) * 512],
                                   ALU.add, [BK[pb_], "xt"], [("x1t", xb_)])
                            P.dma("pool", f"x1s{xb_}", X1[s + b * 128:s + (b + 1) * 128, :], x1t[:, xb_, :],
                                  reads=[("x1t", xb_)], writes=[("X1", t)])
                    P.barrier()
                if stop == 2:
                    raise _StopBuild()

                with contextlib.ExitStack() as p3:
                    def sb3(name, shape, dt):
                        return p3.enter_context(nc.sbuf_tensor(f"{name}{j}", shape, dt))
                    aT = sb3("aT", [128, NFF, 512], BF16)
                    xres = sb3("xres", [128, 4, D], F32)
                    yt = sb3("yt", [128, 4, D], F32)
                    yo = sb3("yo", [128, 4, D], F32)
                    cvall = [sb3(f"cv{i}", [128, 512], F32) for i in range(8)]
                    cwo = PP_CW + (132 if revs[j] else 0)
                    MEMSET("pool", yt[:], 0.0, ["yt"])
                    s = 0
                    while s < nout:
                        no = min(510, nout - s)
                        ni = no + 2
                        lo_tok = max(s - 1, 0)
                        hi_tok = min(s + no + 1, S)
                        c_lo = lo_tok - (s - 1)
                        c_hi = hi_tok - (s - 1)
                        edge = (c_lo > 0) or (c_hi < ni)
                        stage_a(X1, lo_tok, c_lo, c_hi, 1, edge)
                        nb = (no + 127) // 128
                        for b in range(nb):
                            rows = min(128, no - b * 128)
                            P.dma("sp", "xr", xres[0:rows, b, :], X1[s + b * 128:s + b * 128 + rows, :], writes=["xres"])
                        for k in range(11):
                            wu, wuk = load_piece(b_up[:, k * 512:(k + 1) * 512].rearrange("(c p) n -> p c n", p=128),
                                                 wview_d(512))
                            pbase = 4 * (k % 2)
                            for cc in range(4):
                                for dc in range(DC):
                                    MM(bank(pbase + cc, ni), wu[:, dc, cc * 128:(cc + 1) * 128], hT[:, dc, 0:ni],
                                       dc == 0, dc == DC - 1, [wuk, "hT"], [BK[pbase + cc]])
                            for pr in range(2):
                                jf = 2 * k + pr
                                cv = cvall[4 * (jf % 2):4 * (jf % 2) + 4]
                                cvk = [f"cv{4 * (jf % 2) + i}" for i in range(4)]
                                for vi, cc in enumerate((pr, 2 + pr)):
                                    col = 4 * k + cc
                                    bk_ = pbase + cc
                                    u = ps[:, bk_ * 512:bk_ * 512 + 512]
                                    tdst = cv[vi]
                                    ACT(tdst[:, 0:no], u[:, 0:no], AF.Identity, [BK[bk_], "pp"], [cvk[vi]],
                                        scale=pp[:, cwo + col:cwo + col + 1],
                                        bias=pp[:, PP_CB + col:PP_CB + col + 1])
                                    STT("dve", tdst[:, 0:no], u[:, 1:no + 1], pp[:, cwo + 44 + col:cwo + 45 + col],
                                        tdst[:, 0:no], ALU.mult, ALU.add, [BK[bk_], "pp", cvk[vi]], [cvk[vi]])
                                    STT("dve", tdst[:, 0:no], u[:, 2:no + 2], pp[:, cwo + 88 + col:cwo + 89 + col],
                                        tdst[:, 0:no], ALU.mult, ALU.add, [BK[bk_], "pp", cvk[vi]], [cvk[vi]])
                                ACT(cv[2][:, 0:no], cv[1][:, 0:no], AF.Tanh, [cvk[1]], [cvk[2]], scale=0.5)
                                TS("pool", cv[2][:, 0:no], cv[2][:, 0:no], 0.5, 0.5, ALU.mult, ALU.add, [cvk[2]], [cvk[2]])
                                TT("pool", cv[3][:, 0:no], cv[2][:, 0:no], cv[1][:, 0:no], ALU.mult, [cvk[2], cvk[1]], [cvk[3]])
                                TT("pool", aT[:, jf, 0:no], cv[3][:, 0:no], cv[0][:, 0:no], ALU.mult,
                                   [cvk[3], cvk[0]], [("aT", jf)])
                        for nh in range(2):
                            for jh, (j0, jn) in enumerate(((0, 8), (8, 8), (16, 6))):
                                wd, wdk = load_piece(
                                    b_dn[j0 * 128:(j0 + jn) * 128, nh * 512:(nh + 1) * 512]
                                    .rearrange("(c p) n -> p c n", p=128),
                                    lambda tt, jn=jn: tt[:, 0:jn * 512].rearrange("p (c n) -> p c n", c=jn))
                                for b in range(nb):
                                    rows = min(128, no - b * 128)
                                    for jj in range(jn):
                                        jf = j0 + jj
                                        MM(bank(b, 512, rows), aT[:, jf, b * 128:b * 128 + rows], wd[:, jj, :],
                                           jf == 0, jf == NFF - 1, [wdk, ("aT", jf)], [BK[b]])
                            for b in range(nb):
                                rows = min(128, no - b * 128)
                                TT("dve", yt[0:rows, b, nh * 512:(nh + 1) * 512], bank(b, 512, rows),
                                   xres[0:rows, b, nh * 512:(nh + 1) * 512], ALU.add, [BK[b], "xres"], ["yt"])
                        norm_tm(yt, "yt", 2, yo, "yo")
                        for b in range(nb):
                            rows = min(128, no - b * 128)
                            P.dma("pool", "yst", y[s + b * 128:s + b * 128 + rows, :], yo[0:rows, b, :], reads=["yo"],
                                  writes=[("y", j)])
                        s += no
                    P.barrier()
        except _StopBuild:
            pass
        P.limit = stop if (stop is not None and stop > 100) else None
        P.emit()
    return nc


def _rope_tables(pos):
    S = pos.shape[0]
    posf = pos.astype(np.float32)
    out = np.zeros((4, 128, S), np.float32)
    inv_a = (1.0 / (np.float32(500000.0) ** (np.arange(0, 16, 2, dtype=np.float32) / np.float32(16)))).astype(np.float32)
    ang = posf[None, :] * inv_a[:, None]
    ca, sa = np.cos(ang).astype(np.float32), np.sin(ang).astype(np.float32)
    out[0] = 1.0
    for half in range(2):
        b = 64 * half
        out[0, b:b + 8] = ca
        out[0, b + 8:b + 16] = ca
        out[1, b:b + 8] = -sa
        out[1, b + 8:b + 16] = sa
    inv_b = (1.0 / (np.float32(10000.0) ** (np.arange(0, 32, 2, dtype=np.float32) / np.float32(32)))).astype(np.float32)
    rows = (pos // GRID_W).astype(np.float32)
    cols = (pos % GRID_W).astype(np.float32)
    for half in range(2):
        b = 64 * half
        for k, pp_ in enumerate((rows, cols)):
            ang = pp_[None, :] * inv_b[:, None]
            c, s_ = np.cos(ang).astype(np.float32), np.sin(ang).astype(np.float32)
            o = b + 32 * k
            out[2, o:o + 16] = c
            out[2, o + 16:o + 32] = c
            out[3, o:o + 16] = -s_
            out[3, o + 16:o + 32] = s_
    return out


def _const_mats():
    cm = np.zeros((128, NCM, 128), np.float32)
    cm[:, CM_ID, :] = np.eye(128, dtype=np.float32)
    for m in range(128):
        d = m % 64
        if d < 8:
            cm[m + 8, CM_RA, m] = 1.0
        elif d < 16:
            cm[m - 8, CM_RA, m] = 1.0
        dd = d % 32
        if dd < 16:
            cm[m + 16, CM_RB, m] = 1.0
        else:
            cm[m - 16, CM_RB, m] = 1.0
    cm[0:64, CM_S0, :] = 1.0
    cm[64:128, CM_S1, :] = 1.0
    cm[0:64, CM_BLK, 0:64] = 1.0
    cm[64:128, CM_BLK, 64:128] = 1.0
    cm[:, CM_ONE, :] = 1.0
    cm[64, CM_SELB, 0:64] = 1.0
    return cm


def _prep_shared(inp):
    f = lambda a: np.ascontiguousarray(np.asarray(a, dtype=np.float32))
    w_in = f(inp["w_in"])[0]
    qa, ka, va = w_in[:, 0:1024], w_in[:, 1024:2048], w_in[:, 2048:3072]
    qg, kg, vg = w_in[:, 3072:4096], w_in[:, 4096:4352], w_in[:, 4352:4608]
    gates = w_in[:, 4608:6656]
    order = []
    for pair in range(2):
        for i in range(4):
            order += [4 * (2 * pair) + i, 4 * (2 * pair + 1) + i]
    qg_p = np.concatenate([qg[:, h * 64:(h + 1) * 64] for h in order], axis=1)
    w_in_p = np.ascontiguousarray(np.concatenate([qa, qg_p, gates, ka, kg, va, vg], axis=1))
    w_up = f(inp["w_up"])[0]
    perm = []
    for k in range(11):
        for cc in (2 * k, 2 * k + 1):
            perm += list(range(cc * 128, cc * 128 + 128))
        for cc in (2 * k, 2 * k + 1):
            perm += list(range(DFF + cc * 128, DFF + cc * 128 + 128))
    perm = np.array(perm)
    w_up_p = np.ascontiguousarray(w_up[:, perm])
    conv_w = f(inp["conv_w"])[0][:, perm]
    conv_b = f(inp["conv_b"])[0][perm]
    pp = np.zeros((128, NPP), np.float32)
    pp[:, PP_SUBLN] = f(inp["subln_g"])[0]
    pp[:, PP_QN] = np.tile(f(inp["q_norm_g"])[0], 2)
    pp[:, PP_KN] = np.tile(f(inp["k_norm_g"])[0], 2)
    pp[:, PP_EPS] = EPS
    for i, nm in enumerate(("lam_q1", "lam_k1", "lam_q2", "lam_k2")):
        pp[:, PP_LAM + 64 * i:PP_LAM + 64 * (i + 1)] = f(inp[nm])[0][None, :]
    cw = conv_w.reshape(3, 44, 128).transpose(2, 0, 1)
    pp[:, PP_CW:PP_CW + 132] = cw.reshape(128, 132)
    pp[:, PP_CW + 132:PP_CW + 264] = cw[:, ::-1, :].reshape(128, 132)
    pp[:, PP_CB:PP_CB + 44] = conv_b.reshape(44, 128).T
    gvec = np.stack([np.broadcast_to(f(inp["norm_mix_g"])[0], (128, D)),
                     np.broadcast_to(f(inp["norm_ffn_g"])[0], (128, D)),
                     np.broadcast_to(f(inp["norm_final_g"]), (128, D))]).astype(np.float32)
    return dict(w_in=w_in_p, w_pa=f(inp["w_proj_a"])[0], w_pb=f(inp["w_proj_b"])[0], w_o=f(inp["w_out"])[0],
                w_up=w_up_p, w_dn=f(inp["w_down"])[0], gvec=np.ascontiguousarray(gvec), pp=pp, cm=_const_mats())


_CACHE = {}


def kernel(**inputs):
    S = inputs["x_prompt"].shape[1]
    xp = np.asarray(inputs["x_prompt"], dtype=np.float32)
    xsm = np.asarray(inputs["x_sample"], dtype=np.float32)
    seqs = [xp[i] for i in range(xp.shape[0])] + [xsm[i] for i in range(xsm.shape[0])]
    nseq = len(seqs)
    assert nseq == 12 and S % 1024 == 0
    H = S // 2
    jobs = [(S // 512, S), (H // 512 + 1, H)]
    key = (S, tuple(jobs))
    if key not in _CACHE:
        _CACHE[key] = build_program(S, jobs, revs=(False, True))
    nc = _CACHE[key]
    shared = _prep_shared(inputs)
    tab_f = _rope_tables(np.arange(S))
    tab_r = _rope_tables(np.arange(S)[::-1].copy())
    pp_even = shared["pp"].copy()
    pp_even[:, PP_CW + 132:PP_CW + 264] = pp_even[:, PP_CW:PP_CW + 132]
    pp_odd = shared["pp"]
    in_maps = []
    for c in range(N_CORES):
        p, odd = c // 2, c % 2
        m = dict(shared)
        m["x0"] = np.ascontiguousarray(seqs[3 * p + odd])
        sh = seqs[3 * p + 2]
        m["x1"] = np.ascontiguousarray(sh[::-1]) if odd else np.ascontiguousarray(sh)
        m["tab0"] = tab_f
        m["tab1"] = tab_r if odd else tab_f
        m["pp"] = pp_odd if odd else pp_even
        in_maps.append(m)
    res = run_bass_kernel_spmd(nc, in_maps, core_ids=list(range(N_CORES)))
    outs = [None] * nseq
    for p in range(N_CORES // 2):
        outs[3 * p] = res.results[2 * p]["y0"]
        outs[3 * p + 1] = res.results[2 * p + 1]["y0"]
        lo = res.results[2 * p]["y1"]
        hi = res.results[2 * p + 1]["y1"][::-1]
        outs[3 * p + 2] = np.concatenate([lo, hi], axis=0)
    nb = xp.shape[0]
    y_prompt = np.stack(outs[:nb]).astype(np.float32)
    y_sample = np.stack(outs[nb:]).astype(np.float32)
    return (y_prompt, y_sample)
```

```python
import contextlib
import numpy as np
import concourse.bass as bass
import concourse.mybir as mybir
from concourse.bass_utils import run_bass_kernel_spmd

F32 = mybir.dt.float32
BF16 = mybir.dt.bfloat16
AF = mybir.ActivationFunctionType
ALU = mybir.AluOpType
AX = mybir.AxisListType

D = 1024
DC = 8
DFF = 2816
NFF = 22
D_IN = 6656
EPS = 1e-6
GRID_W = 64
N_CORES = 8
ENGS = ("pe", "act", "dve", "pool", "sp")
SEM_EPOCH = 24000
DUMMY_MM = 0

OQ, OG, OK_, OV = 0, 2048, 4096, 5376


class _StopBuild(Exception):
    pass


class Op:
    __slots__ = ("eng", "fn", "deps", "key", "inc", "needs_inc", "epoch", "value", "idx")


class Prog:
    def __init__(self, nc):
        self.nc = nc
        self.q = {e: [] for e in ENGS}
        self.res_w = {}
        self.res_r = {}
        self.last_by_key = {}
        self.pending_barrier = {}
        self.n_ops = 0
        self.limit = None

    def _record(self, eng, fn, reads, writes, key=None, inc=1):
        op = Op()
        op.eng = eng
        op.fn = fn
        is_dma = key is not None
        op.key = key if is_dma else eng
        op.inc = inc
        op.needs_inc = is_dma
        op.epoch = 0
        op.value = 0
        op.idx = self.n_ops
        self.n_ops += 1
        deps = {}
        me = (eng, key) if is_dma else eng
        for r in reads:
            w = self.res_w.get(r)
            if w:
                for o in w.values():
                    deps[id(o)] = o
        for r in writes:
            rd = self.res_r.get(r)
            if rd:
                for e, o in rd.items():
                    deps[id(o)] = o
            w = self.res_w.get(r)
            if w:
                for e, o in w.items():
                    if e != me or is_dma or eng != "pe":
                        deps[id(o)] = o
        pb = self.pending_barrier.pop(eng, None)
        if pb:
            for o in pb:
                deps[id(o)] = o
        op.deps = list(deps.values())
        for o in op.deps:
            o.needs_inc = True
        for r in reads:
            self.res_r.setdefault(r, {})[me] = op
        for r in writes:
            if self.res_r.get(r):
                self.res_w[r] = {me: op}
                self.res_r[r] = {}
            else:
                self.res_w.setdefault(r, {})[me] = op
        self.q[eng].append(op)
        self.last_by_key[op.key] = op
        return op

    def op(self, eng, fn, reads=(), writes=()):
        return self._record(eng, fn, reads, writes)

    def dma(self, eng, key, out, in_, reads=(), writes=()):
        return self._record(eng, lambda e: e.dma_start(out=out, in_=in_), reads, writes,
                            key=("dma", key), inc=16)

    def barrier(self):
        lasts = list(self.last_by_key.values())
        for e in ENGS:
            self.pending_barrier[e] = list(lasts)

    def emit(self):
        nc = self.nc
        if self.limit is not None:
            for e in ENGS:
                self.q[e] = [o for o in self.q[e] if o.idx < self.limit]
        print("n_ops", self.n_ops, {e: len(self.q[e]) for e in ENGS})
        keycount = {}
        sems = {}
        for e in ENGS:
            for op in self.q[e]:
                if not op.needs_inc:
                    continue
                ep, cnt = keycount.get(op.key, (0, 0))
                if cnt + op.inc > SEM_EPOCH:
                    ep, cnt = ep + 1, 0
                cnt += op.inc
                keycount[op.key] = (ep, cnt)
                op.epoch, op.value = ep, cnt
                sems[(op.key, ep)] = None
        last = {}
        for e in ENGS:
            for op in self.q[e]:
                if op.needs_inc:
                    k = (op.key, op.epoch)
                    last[k] = max(last.get(k, 0), op.value)
        with contextlib.ExitStack() as st:
            for i, k in enumerate(list(sems.keys())):
                sems[k] = st.enter_context(nc.semaphore(f"s{i}"))
            block = st.enter_context(nc.Block())

            def run(name, eng):
                waited = {}
                for op in self.q[name]:
                    need = {}
                    for d in op.deps:
                        k = (d.key, d.epoch)
                        if waited.get(k, 0) >= d.value:
                            continue
                        if need.get(k, 0) < d.value:
                            need[k] = d.value
                    for k, v in need.items():
                        eng.wait_ge(sems[k], v)
                        waited[k] = v
                    ins = op.fn(eng)
                    if op.needs_inc:
                        ins.then_inc(sems[(op.key, op.epoch)], op.inc)
                if name == "sp":
                    for k, v in last.items():
                        if waited.get(k, 0) < v:
                            eng.wait_ge(sems[k], v)

            @block.tensor
            def _(eng):
                run("pe", eng)

            @block.scalar
            def _(eng):
                run("act", eng)

            @block.vector
            def _(eng):
                run("dve", eng)

            @block.gpsimd
            def _(eng):
                run("pool", eng)

            @block.sync
            def _(eng):
                run("sp", eng)


PP_SUBLN, PP_QN, PP_KN, PP_EPS, PP_LAM = 0, 1, 2, 3, 4
PP_CW = 4 + 256
PP_CB = PP_CW + 2 * 3 * 44
NPP = PP_CB + 44
CM_ID, CM_RA, CM_RB, CM_S0, CM_S1, CM_BLK, CM_ONE, CM_SELB = range(8)
NCM = 8


def build_program(S, jobs, revs=(False, False), stop=None):
    NT = S // 512
    KC = S // 128
    SEG = min(1024, S)
    NSEG = S // SEG
    SEGC = SEG // 128
    nc = bass.Bass("TRN2", target_bir_lowering=False)
    nj = len(jobs)

    def dram(name, shape, dt, kind):
        return nc.dram_tensor(name, shape, dt, kind=kind).ap()

    xs = [dram(f"x{j}", [S, D], F32, "ExternalInput") for j in range(nj)]
    tabs = [dram(f"tab{j}", [4, 128, S], F32, "ExternalInput") for j in range(nj)]
    ys = [dram(f"y{j}", [jobs[j][1], D], F32, "ExternalOutput") for j in range(nj)]
    w_in = dram("w_in", [D, D_IN], F32, "ExternalInput")
    w_pa = dram("w_pa", [D, D], F32, "ExternalInput")
    w_pb = dram("w_pb", [D, D], F32, "ExternalInput")
    w_o = dram("w_o", [D, D], F32, "ExternalInput")
    w_up = dram("w_up", [D, 2 * DFF], F32, "ExternalInput")
    w_dn = dram("w_dn", [DFF, D], F32, "ExternalInput")
    gvec = dram("gvec", [3, 128, D], F32, "ExternalInput")
    ppd = dram("pp", [128, NPP], F32, "ExternalInput")
    cmd = dram("cm", [128, NCM, 128], F32, "ExternalInput")
    b_in = dram("b_in", [D, D_IN], BF16, "Internal")
    b_pa = dram("b_pa", [D, D], BF16, "Internal")
    b_pb = dram("b_pb", [D, D], BF16, "Internal")
    b_o = dram("b_o", [D, D], BF16, "Internal")
    b_up = dram("b_up", [D, 2 * DFF], BF16, "Internal")
    b_dn = dram("b_dn", [DFF, D], BF16, "Internal")
    KT = dram("KT", [10, 128, S], BF16, "Internal")
    VA = dram("VA", [8, S, 128], BF16, "Internal")
    VG = dram("VG", [4, S, 64], BF16, "Internal")
    X1 = dram("X1", [S, D], F32, "Internal")

    P = Prog(nc)
    st = contextlib.ExitStack()
    with st:
        def sb(name, shape, dt):
            return st.enter_context(nc.sbuf_tensor("s_" + name, shape, dt))

        pp = sb("pp", [128, NPP], F32)
        cmf = sb("cmf", [128, NCM, 128], F32)
        cmb = sb("cmb", [128, NCM, 128], BF16)
        gv = sb("gv", [128, 3, D], F32)
        lamt = sb("lamt", [128, 8], F32)
        mk = sb("mk", [128, 32], F32)
        mke = sb("mke", [128, 32], F32)
        mq = sb("mq", [128, 32], F32)
        negc = sb("negc", [128, 32], F32)
        negcu = sb("negcu", [128, 16], F32)
        small = sb("small", [128, 16], F32)
        mk_tmp = sb("mk_tmp", [128, 8], F32)
        xt = sb("xt", [128, 4, D], F32)
        hb = sb("hb", [128, 4, D], BF16)
        hT = sb("hT", [128, DC, 512], BF16)
        NWB = 3
        wbuf = [sb(f"wbuf{i}", [128, 4096], BF16) for i in range(NWB)]
        tb = sb("tb", [128, 4, 512], F32)
        NSET = 3
        tmpf = [sb(f"tmpf{i}", [128, 512], F32) for i in range(4 * NSET)]
        tmpb = [sb(f"tmpb{i}", [128, 512], BF16) for i in range(3 * NSET)]
        rot = {"i": 0}
        SCR = [(0, 1), (2, 3), (6, 7)]
        ps = st.enter_context(nc.psum_tensor("ps", [128, 4096], F32))

        def bank(i, n=512, parts=128):
            return ps[0:parts, i * 512:i * 512 + n]

        BK = [f"B{i}" for i in range(8)]
        ident = cmb[:, CM_ID, :]

        def MM(out, lhsT, rhs, start, stop, r, w):
            P.op("pe", lambda e: e.matmul(out, lhsT=lhsT, rhs=rhs, start=start, stop=stop), r, w)

        def TR(out, in_, r, w):
            P.op("pe", lambda e: e.transpose(out=out, in_=in_, identity=ident), r, w)

        def ACT(out, in_, func, r, w, scale=1.0, bias=None, accum=None):
            kw = {}
            if bias is not None:
                kw["bias"] = bias
            if accum is not None:
                kw["accum_out"] = accum
            P.op("act", lambda e: e.activation(out=out, in_=in_, func=func, scale=scale, **kw), r, w)

        def TT(eng, out, in0, in1, op, r, w):
            P.op(eng, lambda e: e.tensor_tensor(out=out, in0=in0, in1=in1, op=op), r, w)

        def STT(eng, out, in0, scalar, in1, op0, op1, r, w):
            P.op(eng, lambda e: e.scalar_tensor_tensor(out=out, in0=in0, scalar=scalar, in1=in1,
                                                       op0=op0, op1=op1), r, w)

        def TS(eng, out, in0, s1, s2, op0, op1, r, w):
            P.op(eng, lambda e: e.tensor_scalar(out=out, in0=in0, scalar1=s1, scalar2=s2,
                                                op0=op0, op1=op1), r, w)

        def TS1(eng, out, in0, s1, op, r, w):
            P.op(eng, lambda e: e.tensor_single_scalar(out=out, in_=in0, scalar=s1, op=op), r, w)

        def CP(eng, out, in_, r, w):
            if eng == "act":
                P.op("act", lambda e: e.copy(out=out, in_=in_), r, w)
            else:
                P.op(eng, lambda e: e.tensor_copy(out=out, in_=in_), r, w)

        def MEMSET(eng, ap, val, w):
            P.op(eng, lambda e: e.memset(ap, val), (), w)

        def RMAX(out, in_, r, w):
            P.op("dve", lambda e: e.reduce_max(out=out, in_=in_, axis=AX.X), r, w)

        def RSUM(out, in_, r, w):
            P.op("dve", lambda e: e.reduce_sum(out=out, in_=in_, axis=AX.X), r, w)

        def RECIP(out, in_, r, w):
            P.op("dve", lambda e: e.reciprocal(out=out, in_=in_), r, w)

        dmaq = ["sp", "pool"]

        P.dma("sp", "c0", pp[:], ppd[:, :], writes=["pp"])
        P.dma("sp", "c1", cmf[:], cmd[:, :, :], writes=["cmf"])
        P.dma("sp", "c2", gv[:], gvec.rearrange("g p d -> p g d"), writes=["gv"])
        CP("dve", cmb[:], cmf[:], ["cmf"], ["cmb"])
        for i in range(2):
            TT("dve", tmpf[0][:, 0:64], pp[:, PP_LAM + 128 * i:PP_LAM + 128 * i + 64],
               pp[:, PP_LAM + 128 * i + 64:PP_LAM + 128 * i + 128], ALU.mult, ["pp"], ["tmpf0"])
            RSUM(lamt[:, i:i + 1], tmpf[0][:, 0:64], ["tmpf0"], ["lamt"])
        ACT(lamt[:, 2:4], lamt[:, 0:2], AF.Exp, ["lamt"], ["lamt"])
        TT("dve", lamt[:, 4:5], lamt[:, 3:4], lamt[:, 2:3], ALU.subtract, ["lamt"], ["lamt"])
        TS1("dve", lamt[:, 5:6], lamt[:, 4:5], -0.2, ALU.add, ["lamt"], ["lamt"])
        TS1("dve", lamt[:, 6:7], pp[:, PP_SUBLN:PP_SUBLN + 1], 0.8, ALU.mult, ["pp", "lamt"], ["lamt"])
        neglam = lamt[:, 5:6]
        gsub = lamt[:, 6:7]
        eps_ap = pp[:, PP_EPS:PP_EPS + 1]

        wi = 0
        for (src, dst, rows, cols) in ((w_in, b_in, D, D_IN), (w_pa, b_pa, D, D), (w_pb, b_pb, D, D),
                                       (w_o, b_o, D, D), (w_up, b_up, D, 2 * DFF), (w_dn, b_dn, DFF, D)):
            for r0 in range(0, rows, 128):
                for c0 in range(0, cols, 4096):
                    cw = min(4096, cols - c0)
                    slot = wi % 3
                    P.dma("pool", f"wc{slot}", wbuf[slot][:, 0:cw], src[r0:r0 + 128, c0:c0 + cw],
                          writes=[f"wbuf{slot}"])
                    P.dma("sp", f"ws{slot}", dst[r0:r0 + 128, c0:c0 + cw], wbuf[slot][:, 0:cw],
                          reads=[f"wbuf{slot}"], writes=["wscratch"])
                    wi += 1
        P.barrier()
        if stop == 0:
            jobs = []

        wstate = {"i": 0}

        def load_piece(src_ap, view):
            slot = wstate["i"] % NWB
            wstate["i"] += 1
            dst = view(wbuf[slot])
            P.dma("sp", f"wl{slot}", dst, src_ap, reads=["wscratch"], writes=[f"wbuf{slot}"])
            return dst, f"wbuf{slot}"

        def wview_d(cols):
            return lambda t: t[:, 0:DC * cols].rearrange("p (c n) -> p c n", c=DC)

        def piece_in(c0, cols):
            return load_piece(b_in[:, c0:c0 + cols].rearrange("(c p) n -> p c n", p=128), wview_d(cols))

        def stage_a(src, row0, c_lo, c_hi, gi, zero_first):
            if zero_first:
                MEMSET("pool", xt[:], 0.0, ["xt"])
            for b in range(4):
                lo, hi = max(c_lo, 128 * b), min(c_hi, 128 * b + 128)
                if lo >= hi:
                    continue
                P.dma("sp", "xl", xt[lo - 128 * b:hi - 128 * b, b, :],
                      src[row0 + lo - c_lo:row0 + hi - c_lo, :], writes=["xt"])
            norm_tm(xt, "xt", gi, hb, "hb")
            nb = (c_hi + 127) // 128
            psT = ps[:, 0:2048].bitcast(BF16).rearrange("p (c n) -> p c n", c=DC)
            for b in range(nb):
                for c in range(DC):
                    TR(psT[:, c, b * 128:(b + 1) * 128], hb[:, b, c * 128:(c + 1) * 128],
                       ["hb", "cmb"], [BK[c // 2]])
            w = nb * 128
            for c2 in range(4):
                eng = "dve" if c2 % 2 == 0 else "act"
                CP(eng, hT[:, 2 * c2:2 * c2 + 2, 0:w], psT[:, 2 * c2:2 * c2 + 2, 0:w], [BK[c2]], ["hT"])

        def norm_tm(src_t, src_key, gi, out_t, out_key):
            for b in range(4):
                ACT(hb[:, b, :], src_t[:, b, :], AF.Square, [src_key], ["hb", "ss"], accum=small[:, b:b + 1])
            ACT(small[:, 4:8], small[:, 0:4], AF.Ln, ["ss"], ["ss2"], scale=1.0 / D, bias=eps_ap)
            ACT(small[:, 8:12], small[:, 4:8], AF.Exp, ["ss2"], ["rstd"], scale=-0.5)
            for b in range(4):
                STT("dve", out_t[:, b, :], src_t[:, b, :], small[:, 8 + b:9 + b],
                    gv[:, gi, :], ALU.mult, ALU.mult, [src_key, "rstd", "gv", "hb"], [out_key])

        def finish_qk(psq_bank, is_b, gcol, scale, out_ap, out_key, bound_dst, bound_col, running, W=512):
            st_ = rot["i"] % NSET
            rot["i"] += 1
            tf = [tmpf[4 * st_ + i] for i in range(4)]
            tfk = [f"tmpf{4 * st_ + i}" for i in range(4)]
            tbb = [tmpb[3 * st_ + i] for i in range(3)]
            tbk = [f"tmpb{3 * st_ + i}" for i in range(3)]
            b0, b1 = SCR[st_]
            if is_b:
                ACT(tbb[0][:, 0:W], bank(psq_bank, W), AF.Square, [BK[psq_bank]], [tbk[0]])
                CP("dve", tf[0][:, 0:W], bank(psq_bank, W), [BK[psq_bank], tbk[0]], [tfk[0]])
                MM(bank(b0, W), cmb[:, CM_BLK, :], tbb[0][:, 0:W], True, True, [tbk[0], "cmb"], [BK[b0]])
                ACT(tf[1][:, 0:W], bank(b0, W), AF.Ln, [BK[b0]], [tfk[1]], scale=1.0 / 64, bias=eps_ap)
                ACT(tf[1][:, 0:W], tf[1][:, 0:W], AF.Exp, [tfk[1]], [tfk[1]], scale=-0.5)
                STT("dve", tbb[1][:, 0:W], tf[0][:, 0:W], pp[:, gcol:gcol + 1], tf[1][:, 0:W], ALU.mult, ALU.mult,
                    [tfk[0], tfk[1], "pp"], [tbk[1]])
                rm, ci, si = CM_RB, 2, 3
            else:
                CP("act", tbb[1][:, 0:W], bank(psq_bank, W), [BK[psq_bank]], [tbk[1]])
                rm, ci, si = CM_RA, 0, 1
            MM(bank(b1, W), cmb[:, rm, :], tbb[1][:, 0:W], True, True, [tbk[1], "cmb"], [BK[b1]])
            if scale == 1.0:
                TT("pool", tf[2][:, 0:W], tbb[1][:, 0:W], tb[:, ci, 0:W], ALU.mult, [tbk[1], "tb"], [tfk[2]])
                TT("dve", tf[3][:, 0:W], bank(b1, W), tb[:, si, 0:W], ALU.mult, [BK[b1], "tb"], [tfk[3]])
            else:
                STT("dve", tf[2][:, 0:W], tbb[1][:, 0:W], scale, tb[:, ci, 0:W], ALU.mult, ALU.mult, [tbk[1], "tb"], [tfk[2]])
                STT("dve", tf[3][:, 0:W], bank(b1, W), scale, tb[:, si, 0:W], ALU.mult, ALU.mult, [BK[b1], "tb"], [tfk[3]])
            TT("pool", out_ap, tf[2][:, 0:W], tf[3][:, 0:W], ALU.add, [tfk[2], tfk[3]], [out_key])
            TT("pool", tbb[2][:, 0:W], out_ap, out_ap, ALU.mult, [out_key], [tbk[2]])
            for h in range(2):
                bb = (b0, b1)[h]
                MM(bank(bb, W), cmb[:, CM_S0 + h, :], tbb[2][:, 0:W], True, True, [tbk[2], "cmb"], [BK[bb]])
                if running:
                    sc = small[:, 12 + 2 * st_ + h - 0:13 + 2 * st_ + h - 0] if False else mk_tmp[:, 2 * st_ + h:2 * st_ + h + 1]
                    RMAX(sc, bank(bb, W), [BK[bb]], [("mkt", st_, h)])
                    TT("dve", bound_dst[:, bound_col + h:bound_col + h + 1], bound_dst[:, bound_col + h:bound_col + h + 1],
                       sc, ALU.max, [("mkt", st_, h), ("mkq", bound_col + h)], [("mkq", bound_col + h)])
                else:
                    RMAX(bound_dst[:, bound_col + h:bound_col + h + 1], bank(bb, W), [BK[bb]], [("mkq", bound_col + h)])

        try:
            for j, (nq, nout) in enumerate(jobs):
                x, tab, y = xs[j], tabs[j], ys[j]
                with contextlib.ExitStack() as p1:
                    wk = p1.enter_context(nc.sbuf_tensor(f"wk{j}", [128, DC, 1280], BF16))
                    wv = p1.enter_context(nc.sbuf_tensor(f"wv{j}", [128, DC, 1280], BF16))
                    kout = [p1.enter_context(nc.sbuf_tensor(f"kout{j}_{i}", [128, 512], BF16)) for i in range(3)]
                    vsb = [p1.enter_context(nc.sbuf_tensor(f"vsb{j}_{i}", [128, 1280], BF16)) for i in range(2)]
                    P.dma("sp", "wk", wk[:], b_in[:, OK_:OK_ + 1280].rearrange("(c p) n -> p c n", p=128),
                          reads=["wscratch"], writes=["wk"])
                    P.dma("sp", "wv", wv[:], b_in[:, OV:OV + 1280].rearrange("(c p) n -> p c n", p=128),
                          reads=["wscratch"], writes=["wv"])
                    MEMSET("dve", mk[:], 0.0, [("mkq", c_) for c_ in range(32)])
                    for t in range(NT):
                        s = t * 512
                        stage_a(x, s, 0, 512, 0, False)
                        if stop == 10:
                            raise _StopBuild()
                        P.dma("sp", "tb", tb[:], tab[:, :, s:s + 512].rearrange("f p n -> p f n"), writes=["tb"])
                        for kc in range(10):
                            if stop == 11 and kc == 1:
                                raise _StopBuild()
                            if stop == 12 and kc == 9:
                                raise _StopBuild()
                            pb_ = 4 + (kc % 2)
                            for dc in range(DC):
                                MM(bank(pb_), wk[:, dc, kc * 128:(kc + 1) * 128], hT[:, dc, :], dc == 0, dc == DC - 1,
                                   ["wk", "hT"], [BK[pb_]])
                            ko = kout[kc % 3]
                            kkey = f"kout{kc % 3}"
                            finish_qk(pb_, kc >= 8, PP_KN, 1.0, ko[:], kkey, mk, 2 * kc, True)
                            P.dma("pool", f"kst{kc % 3}", KT[kc, :, s:s + 512], ko[:], reads=[kkey],
                                  writes=[("KT", kc, s // SEG)])
                        if stop == 13:
                            raise _StopBuild()
                        for b in range(4):
                            vs = vsb[b % 2]
                            vkey = f"vsb{b % 2}"
                            for pi, (c0, cw) in enumerate(((0, 512), (512, 512), (1024, 256))):
                                for dc in range(DC):
                                    MM(bank(pi, cw), hT[:, dc, b * 128:(b + 1) * 128], wv[:, dc, c0:c0 + cw],
                                       dc == 0, dc == DC - 1, ["wv", "hT"], [BK[pi]])
                                CP("act" if pi == 1 else "dve", vs[:, c0:c0 + cw], bank(pi, cw), [BK[pi]], [vkey])
                            r0 = s + b * 128
                            P.dma("pool", f"vst{b % 2}", VA[:, r0:r0 + 128, :].rearrange("h p d -> p h d"),
                                  vs[:, 0:1024].rearrange("p (h d) -> p h d", h=8), reads=[vkey],
                                  writes=[("VA", r0 // SEG)])
                            P.dma("pool", f"vsg{b % 2}", VG[:, r0:r0 + 128, :].rearrange("h p d -> p h d"),
                                  vs[:, 1024:1280].rearrange("p (h d) -> p h d", h=4), reads=[vkey],
                                  writes=[("VG", r0 // SEG)])
                    CP("dve", mke[:, 0:16], mk[:, 0:16], [("mkq", c_) for c_ in range(32)], ["mke"])
                    for jj in range(8):
                        CP("dve", mke[:, 16 + 2 * jj:18 + 2 * jj], mk[:, 16 + 2 * (jj // 4):18 + 2 * (jj // 4)],
                           [("mkq", c_) for c_ in range(32)], ["mke"])
                    P.barrier()
                if stop == 1:
                    raise _StopBuild()

                with contextlib.ExitStack() as p2:
                    def sb2(name, shape, dt):
                        return p2.enter_context(nc.sbuf_tensor(f"{name}{j}", shape, dt))
                    qT = sb2("qT", [128, 16, 512], BF16)
                    kseg = [sb2(f"kseg{i}", [128, SEG], BF16) for i in range(3)]
                    vseg = [sb2(f"vseg{i}", [128, SEGC, 128], BF16) for i in range(3)]
                    vsgg = [sb2(f"vsgg{i}", [128, 2, SEGC, 64], BF16) for i in range(3)]
                    pT = [sb2(f"pT{i}", [128, 1024], BF16) for i in range(3)]
                    osb = [sb2(f"osb{i}", [128, 512], F32) for i in range(2)]
                    oaT = sb2("oaT", [128, 8, 512], BF16)
                    obT = sb2("obT", [64, 16, 512], BF16)
                    mixT = hb[:].rearrange("p b d -> p (b d)").rearrange("p (c n) -> p c n", c=8)
                    x1t = sb2("x1t", [128, 2, D], F32)

                    need_tok = min(S, nout + 1) if nout < S else S
                    for t in range(nq):
                        s = t * 512
                        W = min(512, ((need_tok - s + 127) // 128) * 128)
                        NBW = W // 128
                        stage_a(x, s, 0, W, 0, False)
                        P.dma("sp", "tb", tb[:, :, 0:W], tab[:, :, s:s + W].rearrange("f p n -> p f n"), writes=["tb"])
                        for pc in range(4):
                            wp, wkey = piece_in(OQ + pc * 512, 512)
                            for cc in range(4):
                                c = pc * 4 + cc
                                pb_ = 4 + (c % 2)
                                for dc in range(DC):
                                    MM(bank(pb_, W), wp[:, dc, cc * 128:(cc + 1) * 128], hT[:, dc, 0:W], dc == 0, dc == DC - 1,
                                       [wkey, "hT"], [BK[pb_]])
                                finish_qk(pb_, c >= 8, PP_QN, 0.125, qT[:, c, 0:W], ("qT", c), mq, 2 * c, False, W)
                        TT("dve", negc[:], mq[:], mke[:], ALU.mult, [("mkq", c_) for c_ in range(32)] + ["mke"], ["negc"])
                        ACT(negc[:], negc[:], AF.Ln, ["negc"], ["negc"])
                        ACT(negc[:], negc[:], AF.Exp, ["negc"], ["negc"], scale=0.5)
                        TS1("dve", negc[:], negc[:], -1.0, ALU.mult, ["negc"], ["negc"])

                        units = []
                        for h in range(8):
                            units.append(dict(kind="A", kc=h, qc=h, out=h))
                        for p_ in range(2):
                            for i_ in range(4):
                                units.append(dict(kind="B", kc=8 + p_, qc=8 + 4 * p_ + i_, v=[2 * p_, 2 * p_ + 1],
                                                  out=[4 * (2 * p_) + i_, 4 * (2 * p_ + 1) + i_]))
                        TT("dve", negcu[:], negc[:, 0:32:2], negc[:, 1:32:2], ALU.min, ["negc"], ["negcu"])
                        segs = [(u, sg) for u in range(len(units)) for sg in range(NSEG)]

                        def load_seg(i):
                            u, sg = segs[i]
                            un = units[u]
                            sl = i % 3
                            P.dma("sp", f"ks{sl}", kseg[sl][:], KT[un["kc"], :, sg * SEG:(sg + 1) * SEG],
                                  reads=[("KT", un["kc"], sg)], writes=[f"kseg{sl}"])
                            if un["kind"] == "A":
                                P.dma("sp", f"vs{sl}", vseg[sl][:],
                                      VA[un["kc"], sg * SEG:(sg + 1) * SEG, :].rearrange("(c p) d -> p c d", p=128),
                                      reads=[("VA", sg)], writes=[f"vseg{sl}"])
                            else:
                                for jv in range(2):
                                    P.dma("sp", f"vs{sl}", vsgg[sl][:, jv, :, :],
                                          VG[un["v"][jv], sg * SEG:(sg + 1) * SEG, :].rearrange("(c p) d -> p c d", p=128),
                                          reads=[("VG", sg)], writes=[f"vsgg{sl}"])

                        steps = []
                        for i, (u, sg) in enumerate(segs):
                            for c_ in range(SEGC):
                                steps.append((i, u, sg, c_))
                        nsteps = len(steps)

                        def rec_qk(k):
                            i, u, sg, c_ = steps[k]
                            un = units[u]
                            sl = i % 3
                            stb = k % 2
                            for sj in range(2):
                                r0 = 64 * sj
                                MM(bank(2 * stb + sj, W), kseg[sl][r0:r0 + 64, c_ * 128:(c_ + 1) * 128],
                                   qT[r0:r0 + 64, un["qc"], 0:W], True, True,
                                   [f"kseg{sl}", ("qT", un["qc"])], [BK[2 * stb + sj]])

                        def rec_rest(k):
                            i, u, sg, c_ = steps[k]
                            un = units[u]
                            sl = i % 3
                            stb = k % 2
                            pt = pT[k % 3]
                            ptk = f"pT{k % 3}"
                            qc = un["qc"]
                            ACT(pt[:].rearrange("p (b n) -> p b n", b=2)[:, :, 0:W],
                                ps[:, stb * 1024:(stb + 1) * 1024].rearrange("p (b n) -> p b n", b=2)[:, :, 0:W], AF.Exp, [BK[2 * stb], BK[2 * stb + 1], "negcu"], [ptk],
                                bias=negcu[:, qc:qc + 1])
                            cg = sg * SEGC + c_
                            first = cg == 0
                            last = cg == NSEG * SEGC - 1
                            isA = un["kind"] == "A"
                            for sj in range(2):
                                if isA:
                                    lhsT = vseg[sl][:, c_, :]
                                    out = bank(4 + sj, W)
                                    vk = f"vseg{sl}"
                                else:
                                    lhsT = vsgg[sl][:, sj, c_, :]
                                    out = bank(4 + sj, W, 64)
                                    vk = f"vsgg{sl}"
                                MM(out, lhsT, pt[:, sj * 512:sj * 512 + W], first, last, [vk, ptk], [BK[4 + sj]])
                            for sj in range(2):
                                MM(bank(6 + sj, W), cmb[:, CM_ONE, :], pt[:, sj * 512:sj * 512 + W],
                                   first, last, ["cmb", ptk], [BK[6 + sj]])
                            if last:
                                finalize_pair(u)

                        def finalize_pair(u):
                            un = units[u]
                            isA = un["kind"] == "A"
                            np_ = 128 if isA else 64
                            sets = []
                            for sj in range(2):
                                st_ = rot["i"] % NSET
                                rot["i"] += 1
                                tf = [tmpf[4 * st_ + i] for i in range(4)]
                                tfk = [f"tmpf{4 * st_ + i}" for i in range(4)]
                                sets.append((st_, tf, tfk))
                                CP("dve", tf[3][0:np_, 0:W], bank(4 + sj, W, np_), [BK[4 + sj]], [tfk[3]])
                                ACT(tf[0][0:np_, 0:W], bank(6 + sj, W, np_), AF.Ln, [BK[6 + sj]], [tfk[0]])
                            for sj in range(2):
                                st_, tf, tfk = sets[sj]
                                ACT(tf[0][0:np_, 0:W], tf[0][0:np_, 0:W], AF.Exp, [tfk[0]], [tfk[0]], scale=-1.0)
                                if isA:
                                    TT("dve", osb[sj][:, 0:W], tf[3][:, 0:W], tf[0][:, 0:W], ALU.mult, [tfk[3], tfk[0]], [f"osb{sj}"])
                                else:
                                    hb_ = un["out"][sj]
                                    TT("dve", obT[:, hb_, 0:W], tf[3][0:64, 0:W], tf[0][0:64, 0:W], ALU.mult,
                                       [tfk[3], tfk[0]], [("obT", hb_)])
                            if isA:
                                STT("dve", oaT[:, un["out"], 0:W], osb[1][:, 0:W], neglam, osb[0][:, 0:W], ALU.mult, ALU.add,
                                    ["osb0", "osb1", "lamt"], [("oaT", un["out"])])

                        load_seg(0)
                        if len(segs) > 1:
                            load_seg(1)
                        rec_qk(0)
                        for k in range(nsteps):
                            i, u, sg, c_ = steps[k]
                            if c_ == 0 and i + 2 < len(segs):
                                load_seg(i + 2)
                            if k + 1 < nsteps:
                                rec_qk(k + 1)
                            rec_rest(k)

                        for h in range(8):
                            st_ = rot["i"] % NSET
                            rot["i"] += 1
                            tf = [tmpf[4 * st_ + i] for i in range(4)]
                            tfk = [f"tmpf{4 * st_ + i}" for i in range(4)]
                            tbs, tbsk = tmpb[3 * st_], f"tmpb{3 * st_}"
                            bq = SCR[st_][0]
                            TT("pool", tbs[:, 0:W], oaT[:, h, 0:W], oaT[:, h, 0:W], ALU.mult, [("oaT", h)], [tbsk])
                            MM(bank(bq, W), cmb[:, CM_ONE, :], tbs[:, 0:W], True, True, [tbsk, "cmb"], [BK[bq]])
                            ACT(tf[2][:, 0:W], bank(bq, W), AF.Ln, [BK[bq]], [tfk[2]], scale=1.0 / 128, bias=eps_ap)
                            ACT(tf[2][:, 0:W], tf[2][:, 0:W], AF.Exp, [tfk[2]], [tfk[2]], scale=-0.5)
                            STT("dve", oaT[:, h, 0:W], oaT[:, h, 0:W], gsub, tf[2][:, 0:W], ALU.mult, ALU.mult,
                                [("oaT", h), tfk[2], "lamt"], [("oaT", h)])
                        for m in range(8):
                            slot = wstate["i"] % NWB
                            wstate["i"] += 1
                            tA = wbuf[slot]
                            wkA = f"wbuf{slot}"
                            wa = tA[:, 0:1024].rearrange("p (c n) -> p c n", c=DC)
                            wga = tA[:, 1024:2048].rearrange("p (c n) -> p c n", c=DC)
                            wgb = tA[:, 2048:3072].rearrange("p (c n) -> p c n", c=DC)
                            P.dma("sp", f"wl{slot}", wa, b_pa[:, m * 128:(m + 1) * 128].rearrange("(c p) n -> p c n", p=128),
                                  reads=["wscratch"], writes=[wkA])
                            P.dma("sp", f"wl{slot}", wga,
                                  b_in[:, OG + m * 128:OG + (m + 1) * 128].rearrange("(c p) n -> p c n", p=128),
                                  reads=["wscratch"], writes=[wkA])
                            P.dma("sp", f"wl{slot}", wgb,
                                  b_in[:, OG + 1024 + m * 128:OG + 1024 + (m + 1) * 128].rearrange("(c p) n -> p c n", p=128),
                                  reads=["wscratch"], writes=[wkA])
                            slot = wstate["i"] % NWB
                            wstate["i"] += 1
                            wkB = f"wbuf{slot}"
                            wb_ = wbuf[slot][0:64, 0:2048].rearrange("p (h n) -> p h n", h=16)
                            P.dma("sp", f"wl{slot}", wb_, b_pb[:, m * 128:(m + 1) * 128].rearrange("(h p) n -> p h n", p=64),
                                  reads=["wscratch"], writes=[wkB])
                            st_ = m % NSET
                            tf = [tmpf[4 * st_ + i] for i in range(4)]
                            tfk = [f"tmpf{4 * st_ + i}" for i in range(4)]
                            q0 = 4 * (m % 2)
                            for dc in range(DC):
                                MM(bank(q0, W), wa[:, dc, :], oaT[:, dc, 0:W], dc == 0, dc == DC - 1,
                                   [wkA, ("oaT", dc)], [BK[q0]])
                            for hh in range(16):
                                MM(bank(q0 + 1, W), wb_[:, hh, :], obT[:, hh, 0:W], hh == 0, hh == 15,
                                   [wkB, ("obT", hh)], [BK[q0 + 1]])
                            for dc in range(DC):
                                MM(bank(q0 + 2, W), wga[:, dc, :], hT[:, dc, 0:W], dc == 0, dc == DC - 1, [wkA, "hT"], [BK[q0 + 2]])
                            for dc in range(DC):
                                MM(bank(q0 + 3, W), wgb[:, dc, :], hT[:, dc, 0:W], dc == 0, dc == DC - 1, [wkA, "hT"], [BK[q0 + 3]])
                            ACT(tf[0][:, 0:W], bank(q0 + 2, W), AF.Tanh, [BK[q0 + 2]], [tfk[0]], scale=0.5)
                            ACT(tf[1][:, 0:W], bank(q0 + 3, W), AF.Tanh, [BK[q0 + 3]], [tfk[1]], scale=0.5)
                            TS("pool", tf[0][:, 0:W], tf[0][:, 0:W], 0.5, 0.5, ALU.mult, ALU.add, [tfk[0]], [tfk[0]])
                            TS("pool", tf[1][:, 0:W], tf[1][:, 0:W], 0.5, 0.5, ALU.mult, ALU.add, [tfk[1]], [tfk[1]])
                            TT("dve", tf[2][:, 0:W], tf[0][:, 0:W], bank(q0, W), ALU.mult, [tfk[0], BK[q0]], [tfk[2]])
                            TT("dve", tf[3][:, 0:W], tf[1][:, 0:W], bank(q0 + 1, W), ALU.mult, [tfk[1], BK[q0 + 1]], [tfk[3]])
                            TT("pool", mixT[:, m, 0:W], tf[2][:, 0:W], tf[3][:, 0:W], ALU.add, [tfk[2], tfk[3]], ["hb"])
                        wos = []
                        for nh in range(2):
                            wos.append(load_piece(b_o[:, nh * 512:(nh + 1) * 512].rearrange("(c p) n -> p c n", p=128),
                                                  wview_d(512)))
                        for b in range(NBW):
                            xb_ = b % 2
                            for nh in range(2):
                                wo, wok = wos[nh]
                                pb_ = 4 + 2 * xb_ + nh
                                for dc in range(DC):
                                    MM(bank(pb_), mixT[:, dc, b * 128:(b + 1) * 128], wo[:, dc, :], dc == 0, dc == DC - 1,
                                       [wok, "hb"], [BK[pb_]])
                                TT("dve", x1t[:, xb_, nh * 512:(nh + 1) * 512], bank(pb_), xt[:, b, nh * 512:(nh + 1) * 512],
                                   ALU.add, [BK[pb_], "xt"], [("x1t", xb_)])
                            P.dma("pool", f"x1s{xb_}", X1[s + b * 128:s + (b + 1) * 128, :], x1t[:, xb_, :],
                                  reads=[("x1t", xb_)], writes=[("X1", t)])
                    P.barrier()
                if stop == 2:
                    raise _StopBuild()

                with contextlib.ExitStack() as p3:
                    def sb3(name, shape, dt):
                        return p3.enter_context(nc.sbuf_tensor(f"{name}{j}", shape, dt))
                    aT = sb3("aT", [128, NFF, 512], BF16)
                    xres = sb3("xres", [128, 4, D], F32)
                    yt = sb3("yt", [128, 4, D], F32)
                    yo = sb3("yo", [128, 4, D], F32)
                    cvall = [sb3(f"cv{i}", [128, 512], F32) for i in range(8)]
                    cwo = PP_CW + (132 if revs[j] else 0)
                    MEMSET("pool", yt[:], 0.0, ["yt"])
                    s = 0
                    while s < nout:
                        no = min(510, nout - s)
                        ni = no + 2
                        lo_tok = max(s - 1, 0)
                        hi_tok = min(s + no + 1, S)
                        c_lo = lo_tok - (s - 1)
                        c_hi = hi_tok - (s - 1)
                        edge = (c_lo > 0) or (c_hi < ni)
                        stage_a(X1, lo_tok, c_lo, c_hi, 1, edge)
                        nb = (no + 127) // 128
                        for b in range(nb):
                            rows = min(128, no - b * 128)
                            P.dma("sp", "xr", xres[0:rows, b, :], X1[s + b * 128:s + b * 128 + rows, :], writes=["xres"])
                        for k in range(11):
                            wu, wuk = load_piece(b_up[:, k * 512:(k + 1) * 512].rearrange("(c p) n -> p c n", p=128),
                                                 wview_d(512))
                            pbase = 4 * (k % 2)
                            for cc in range(4):
                                for dc in range(DC):
                                    MM(bank(pbase + cc, ni), wu[:, dc, cc * 128:(cc + 1) * 128], hT[:, dc, 0:ni],
                                       dc == 0, dc == DC - 1, [wuk, "hT"], [BK[pbase + cc]])
                            for pr in range(2):
                                jf = 2 * k + pr
                                cv = cvall[4 * (jf % 2):4 * (jf % 2) + 4]
                                cvk = [f"cv{4 * (jf % 2) + i}" for i in range(4)]
                                for vi, cc in enumerate((pr, 2 + pr)):
                                    col = 4 * k + cc
                                    bk_ = pbase + cc
                                    u = ps[:, bk_ * 512:bk_ * 512 + 512]
                                    tdst = cv[vi]
                                    ACT(tdst[:, 0:no], u[:, 0:no], AF.Identity, [BK[bk_], "pp"], [cvk[vi]],
                                        scale=pp[:, cwo + col:cwo + col + 1],
                                        bias=pp[:, PP_CB + col:PP_CB + col + 1])
                                    STT("dve", tdst[:, 0:no], u[:, 1:no + 1], pp[:, cwo + 44 + col:cwo + 45 + col],
                                        tdst[:, 0:no], ALU.mult, ALU.add, [BK[bk_], "pp", cvk[vi]], [cvk[vi]])
                                    STT("dve", tdst[:, 0:no], u[:, 2:no + 2], pp[:, cwo + 88 + col:cwo + 89 + col],
                                        tdst[:, 0:no], ALU.mult, ALU.add, [BK[bk_], "pp", cvk[vi]], [cvk[vi]])
                                ACT(cv[2][:, 0:no], cv[1][:, 0:no], AF.Tanh, [cvk[1]], [cvk[2]], scale=0.5)
                                TS("pool", cv[2][:, 0:no], cv[2][:, 0:no], 0.5, 0.5, ALU.mult, ALU.add, [cvk[2]], [cvk[2]])
                                TT("pool", cv[3][:, 0:no], cv[2][:, 0:no], cv[1][:, 0:no], ALU.mult, [cvk[2], cvk[1]], [cvk[3]])
                                TT("pool", aT[:, jf, 0:no], cv[3][:, 0:no], cv[0][:, 0:no], ALU.mult,
                                   [cvk[3], cvk[0]], [("aT", jf)])
                        for nh in range(2):
                            for jh, (j0, jn) in enumerate(((0, 8), (8, 8), (16, 6))):
                                wd, wdk = load_piece(
                                    b_dn[j0 * 128:(j0 + jn) * 128, nh * 512:(nh + 1) * 512]
                                    .rearrange("(c p) n -> p c n", p=128),
                                    lambda tt, jn=jn: tt[:, 0:jn * 512].rearrange("p (c n) -> p c n", c=jn))
                                for b in range(nb):
                                    rows = min(128, no - b * 128)
                                    for jj in range(jn):
                                        jf = j0 + jj
                                        MM(bank(b, 512, rows), aT[:, jf, b * 128:b * 128 + rows], wd[:, jj, :],
                                           jf == 0, jf == NFF - 1, [wdk, ("aT", jf)], [BK[b]])
                            for b in range(nb):
                                rows = min(128, no - b * 128)
                                TT("dve", yt[0:rows, b, nh * 512:(nh + 1) * 512], bank(b, 512, rows),
                                   xres[0:rows, b, nh * 512:(nh + 1) * 512], ALU.add, [BK[b], "xres"], ["yt"])
                        norm_tm(yt, "yt", 2, yo, "yo")
                        for b in range(nb):
                            rows = min(128, no - b * 128)
                            P.dma("pool", "yst", y[s + b * 128:s + b * 128 + rows, :], yo[0:rows, b, :], reads=["yo"],
                                  writes=[("y", j)])
                        s += no
                    P.barrier()
        except _StopBuild:
            pass
        P.limit = stop if (stop is not None and stop > 100) else None
        P.emit()
    return nc


def _rope_tables(pos):
    S = pos.shape[0]
    posf = pos.astype(np.float32)
    out = np.zeros((4, 128, S), np.float32)
    inv_a = (1.0 / (np.float32(500000.0) ** (np.arange(0, 16, 2, dtype=np.float32) / np.float32(16)))).astype(np.float32)
    ang = posf[None, :] * inv_a[:, None]
    ca, sa = np.cos(ang).astype(np.float32), np.sin(ang).astype(np.float32)
    out[0] = 1.0
    for half in range(2):
        b = 64 * half
        out[0, b:b + 8] = ca
        out[0, b + 8:b + 16] = ca
        out[1, b:b + 8] = -sa
        out[1, b + 8:b + 16] = sa
    inv_b = (1.0 / (np.float32(10000.0) ** (np.arange(0, 32, 2, dtype=np.float32) / np.float32(32)))).astype(np.float32)
    rows = (pos // GRID_W).astype(np.float32)
    cols = (pos % GRID_W).astype(np.float32)
    for half in range(2):
        b = 64 * half
        for k, pp_ in enumerate((rows, cols)):
            ang = pp_[None, :] * inv_b[:, None]
            c, s_ = np.cos(ang).astype(np.float32), np.sin(ang).astype(np.float32)
            o = b + 32 * k
            out[2, o:o + 16] = c
            out[2, o + 16:o + 32] = c
            out[3, o:o + 16] = -s_
            out[3, o + 16:o + 32] = s_
    return out


def _const_mats():
    cm = np.zeros((128, NCM, 128), np.float32)
    cm[:, CM_ID, :] = np.eye(128, dtype=np.float32)
    for m in range(128):
        d = m % 64
        if d < 8:
            cm[m + 8, CM_RA, m] = 1.0
        elif d < 16:
            cm[m - 8, CM_RA, m] = 1.0
        dd = d % 32
        if dd < 16:
            cm[m + 16, CM_RB, m] = 1.0
        else:
            cm[m - 16, CM_RB, m] = 1.0
    cm[0:64, CM_S0, :] = 1.0
    cm[64:128, CM_S1, :] = 1.0
    cm[0:64, CM_BLK, 0:64] = 1.0
    cm[64:128, CM_BLK, 64:128] = 1.0
    cm[:, CM_ONE, :] = 1.0
    cm[64, CM_SELB, 0:64] = 1.0
    return cm


def _prep_shared(inp):
    f = lambda a: np.ascontiguousarray(np.asarray(a, dtype=np.float32))
    w_in = f(inp["w_in"])[0]
    qa, ka, va = w_in[:, 0:1024], w_in[:, 1024:2048], w_in[:, 2048:3072]
    qg, kg, vg = w_in[:, 3072:4096], w_in[:, 4096:4352], w_in[:, 4352:4608]
    gates = w_in[:, 4608:6656]
    order = []
    for pair in range(2):
        for i in range(4):
            order += [4 * (2 * pair) + i, 4 * (2 * pair + 1) + i]
    qg_p = np.concatenate([qg[:, h * 64:(h + 1) * 64] for h in order], axis=1)
    w_in_p = np.ascontiguousarray(np.concatenate([qa, qg_p, gates, ka, kg, va, vg], axis=1))
    w_up = f(inp["w_up"])[0]
    perm = []
    for k in range(11):
        for cc in (2 * k, 2 * k + 1):
            perm += list(range(cc * 128, cc * 128 + 128))
        for cc in (2 * k, 2 * k + 1):
            perm += list(range(DFF + cc * 128, DFF + cc * 128 + 128))
    perm = np.array(perm)
    w_up_p = np.ascontiguousarray(w_up[:, perm])
    conv_w = f(inp["conv_w"])[0][:, perm]
    conv_b = f(inp["conv_b"])[0][perm]
    pp = np.zeros((128, NPP), np.float32)
    pp[:, PP_SUBLN] = f(inp["subln_g"])[0]
    pp[:, PP_QN] = np.tile(f(inp["q_norm_g"])[0], 2)
    pp[:, PP_KN] = np.tile(f(inp["k_norm_g"])[0], 2)
    pp[:, PP_EPS] = EPS
    for i, nm in enumerate(("lam_q1", "lam_k1", "lam_q2", "lam_k2")):
        pp[:, PP_LAM + 64 * i:PP_LAM + 64 * (i + 1)] = f(inp[nm])[0][None, :]
    cw = conv_w.reshape(3, 44, 128).transpose(2, 0, 1)
    pp[:, PP_CW:PP_CW + 132] = cw.reshape(128, 132)
    pp[:, PP_CW + 132:PP_CW + 264] = cw[:, ::-1, :].reshape(128, 132)
    pp[:, PP_CB:PP_CB + 44] = conv_b.reshape(44, 128).T
    gvec = np.stack([np.broadcast_to(f(inp["norm_mix_g"])[0], (128, D)),
                     np.broadcast_to(f(inp["norm_ffn_g"])[0], (128, D)),
                     np.broadcast_to(f(inp["norm_final_g"]), (128, D))]).astype(np.float32)
    return dict(w_in=w_in_p, w_pa=f(inp["w_proj_a"])[0], w_pb=f(inp["w_proj_b"])[0], w_o=f(inp["w_out"])[0],
                w_up=w_up_p, w_dn=f(inp["w_down"])[0], gvec=np.ascontiguousarray(gvec), pp=pp, cm=_const_mats())


_CACHE = {}


def kernel(**inputs):
    S = inputs["x_prompt"].shape[1]
    xp = np.asarray(inputs["x_prompt"], dtype=np.float32)
    xsm = np.asarray(inputs["x_sample"], dtype=np.float32)
    seqs = [xp[i] for i in range(xp.shape[0])] + [xsm[i] for i in range(xsm.shape[0])]
    nseq = len(seqs)
    assert nseq == 12 and S % 1024 == 0
    H = S // 2
    jobs = [(S // 512, S), (H // 512 + 1, H)]
    key = (S, tuple(jobs))
    if key not in _CACHE:
        _CACHE[key] = build_program(S, jobs, revs=(False, True))
    nc = _CACHE[key]
    shared = _prep_shared(inputs)
    tab_f = _rope_tables(np.arange(S))
    tab_r = _rope_tables(np.arange(S)[::-1].copy())
    pp_even = shared["pp"].copy()
    pp_even[:, PP_CW + 132:PP_CW + 264] = pp_even[:, PP_CW:PP_CW + 132]
    pp_odd = shared["pp"]
    in_maps = []
    for c in range(N_CORES):
        p, odd = c // 2, c % 2
        m = dict(shared)
        m["x0"] = np.ascontiguousarray(seqs[3 * p + odd])
        sh = seqs[3 * p + 2]
        m["x1"] = np.ascontiguousarray(sh[::-1]) if odd else np.ascontiguousarray(sh)
        m["tab0"] = tab_f
        m["tab1"] = tab_r if odd else tab_f
        m["pp"] = pp_odd if odd else pp_even
        in_maps.append(m)
    res = run_bass_kernel_spmd(nc, in_maps, core_ids=list(range(N_CORES)))
    outs = [None] * nseq
    for p in range(N_CORES // 2):
        outs[3 * p] = res.results[2 * p]["y0"]
        outs[3 * p + 1] = res.results[2 * p + 1]["y0"]
        lo = res.results[2 * p]["y1"]
        hi = res.results[2 * p + 1]["y1"][::-1]
        outs[3 * p + 2] = np.concatenate([lo, hi], axis=0)
    nb = xp.shape[0]
    y_prompt = np.stack(outs[:nb]).astype(np.float32)
    y_sample = np.stack(outs[nb:]).astype(np.float32)
    return (y_prompt, y_sample)
```
